# Optimizing a Trainium2 kernel written in Bass

```python
import jax, jax.numpy as jnp
from jax import lax
import numpy as np

D_MODEL = 2048
BATCH = 8
SEQ = 4096
DEPTH = 1

N_META = 16
BLK = 128
PAD_FRONT = BLK - N_META
MLA_HEADS = 8
MLA_NOPE = 128
MLA_ROPE = 64
MLA_V = 128
MLA_Q_RANK = 512
MLA_KV_RANK = 256
MLA_WIDTH = MLA_HEADS * MLA_V
RET_HEADS = 8
RET_DK = 128
RET_DV = 128
RET_WIDTH = RET_HEADS * RET_DV
ROPE_BASE = 10000.0
NORM_EPS = 1e-6
GN_EPS = 1e-5
N_BRANCH = 2
NEG_INF = -1e30
IN_SPLITS = (MLA_Q_RANK, MLA_KV_RANK, MLA_ROPE, MLA_WIDTH,
             RET_HEADS * RET_DK, RET_HEADS * RET_DK, RET_WIDTH, RET_WIDTH,
             N_BRANCH * D_MODEL)
IN_WIDTH = sum(IN_SPLITS)

kernel_name = "mla_retention_gated_hybrid"


def rmsnorm(x, w):
    xf = x.astype(jnp.float32)
    y = xf * lax.rsqrt(jnp.mean(xf * xf, axis=-1, keepdims=True) + NORM_EPS)
    return (y * w.astype(jnp.float32)).astype(x.dtype)


def rope(t, pos):
    d = t.shape[-1]
    inv = ROPE_BASE ** (-jnp.arange(0, d, 2, dtype=jnp.float32) / d)
    ang = pos.astype(jnp.float32)[:, None] * inv[None, :]
    ang = ang.reshape(ang.shape[:1] + (1,) * (t.ndim - 3) + ang.shape[1:])
    cos = jnp.cos(ang).astype(t.dtype)
    sin = jnp.sin(ang).astype(t.dtype)
    t1, t2 = t[..., : d // 2], t[..., d // 2:]
    return jnp.concatenate([t1 * cos - t2 * sin, t1 * sin + t2 * cos], axis=-1)


def pad_front(t):
    return jnp.pad(t, ((0, 0), (PAD_FRONT, 0)) + ((0, 0),) * (t.ndim - 2))


def mla_branch(c_q, c_kv, k_pe, pos, q_norm_w, w_uq, kv_norm_w, w_ukv):
    B, L, _ = c_q.shape
    q = (rmsnorm(c_q, q_norm_w) @ w_uq).reshape(B, L, MLA_HEADS, MLA_NOPE + MLA_ROPE)
    q_nope, q_pe = q[..., :MLA_NOPE], rope(q[..., MLA_NOPE:], pos)
    kv = (rmsnorm(c_kv, kv_norm_w) @ w_ukv).reshape(B, L, MLA_HEADS, MLA_NOPE + MLA_V)
    k_nope, v = kv[..., :MLA_NOPE], kv[..., MLA_NOPE:]
    k_pe = rope(k_pe, pos)
    q_nope, q_pe, k_nope, v, k_pe = (pad_front(t) for t in (q_nope, q_pe, k_nope, v, k_pe))
    Lp = L + PAD_FRONT
    scale = (MLA_NOPE + MLA_ROPE) ** -0.5
    kidx = jnp.arange(Lp)

    def block(i):
        start = i * BLK
        qn = lax.dynamic_slice_in_dim(q_nope, start, BLK, axis=1)
        qp = lax.dynamic_slice_in_dim(q_pe, start, BLK, axis=1)
        s = (jnp.einsum('bqhd,bkhd->bhqk', qn, k_nope)
             + jnp.einsum('bqhd,bkd->bhqk', qp, k_pe)).astype(jnp.float32) * scale
        qidx = start + jnp.arange(BLK)
        mask = (kidx[None, :] <= qidx[:, None]) & (kidx[None, :] >= PAD_FRONT)
        s = jnp.where(mask[None, None], s, NEG_INF)
        p = jax.nn.softmax(s, axis=-1).astype(v.dtype)
        return jnp.einsum('bhqk,bkhd->bqhd', p, v)

    o = lax.map(block, jnp.arange(Lp // BLK))
    o = o.transpose(1, 0, 2, 3, 4).reshape(B, Lp, MLA_WIDTH)
    return o[:, PAD_FRONT:]


def retention_branch(r_q, r_k, r_v, pos, gn_w, gn_b):
    B, L, _ = r_q.shape
    dt = r_q.dtype
    q = rope(r_q.reshape(B, L, RET_HEADS, RET_DK), pos)
    k = rope(r_k.reshape(B, L, RET_HEADS, RET_DK), pos) * (RET_DK ** -0.5)
    v = r_v.reshape(B, L, RET_HEADS, RET_DV)
    q, k, v = (pad_front(t).astype(jnp.float32) for t in (q, k, v))
    Lp = L + PAD_FRONT
    nc = Lp // BLK

    def chunk(t):
        return t.reshape(B, nc, BLK, RET_HEADS, -1).transpose(0, 3, 1, 2, 4)

    qc, kc, vc = chunk(q), chunk(k), chunk(v)
    log_g = jnp.log1p(-(2.0 ** (-5.0 - jnp.arange(RET_HEADS, dtype=jnp.float32))))
    n = jnp.arange(BLK, dtype=jnp.float32)
    diff = n[:, None] - n[None, :]
    decay_in = jnp.where(diff >= 0, jnp.exp(log_g[:, None, None] * jnp.maximum(diff, 0.0)), 0.0)
    zeta = jnp.exp(log_g[:, None] * (BLK - 1.0 - n))
    xi = jnp.exp(log_g[:, None] * (n + 1.0))
    g_chunk = jnp.exp(log_g * BLK)
    s = jnp.einsum('bhcnd,bhcmd->bhcnm', qc, kc) * decay_in[None, :, None]
    inner = jnp.einsum('bhcnm,bhcme->bhcne', s, vc)
    kv_chunk = jnp.einsum('bhcmd,bhcme->cbhde', kc * zeta[None, :, None, :, None], vc)

    def step(R, kv):
        return R * g_chunk[None, :, None, None] + kv, R

    _, R_prev = lax.scan(step, jnp.zeros((B, RET_HEADS, RET_DK, RET_DV), jnp.float32), kv_chunk)
    cross = jnp.einsum('bhcnd,cbhde->bhcne', qc, R_prev) * xi[None, :, None, :, None]
    o = (inner + cross).transpose(0, 2, 3, 1, 4).reshape(B, Lp, RET_HEADS, RET_DV)[:, PAD_FRONT:]
    mu = jnp.mean(o, axis=-1, keepdims=True)
    var = jnp.mean(jnp.square(o - mu), axis=-1, keepdims=True)
    o = ((o - mu) * lax.rsqrt(var + GN_EPS)).reshape(B, L, RET_WIDTH)
    o = o * gn_w.astype(jnp.float32) + gn_b.astype(jnp.float32)
    return o.astype(dt)


def hybrid_layer(h, pos, norm_w, w_in, mla_q_norm_w, mla_w_uq, mla_kv_norm_w, mla_w_ukv,
                 ret_gn_w, ret_gn_b, w_branch_mla, w_branch_ret, w_out):
    B, L, D = h.shape
    xn = rmsnorm(h, norm_w)
    proj = xn @ w_in
    offs, acc = [], 0
    for w in IN_SPLITS[:-1]:
        acc += w
        offs.append(acc)
    c_q, c_kv, k_pe, z_mla, r_q, r_k, r_v, z_ret, gate_logits = jnp.split(proj, offs, axis=-1)
    y_mla = mla_branch(c_q, c_kv, k_pe, pos, mla_q_norm_w, mla_w_uq, mla_kv_norm_w, mla_w_ukv) * jax.nn.silu(z_mla)
    y_ret = retention_branch(r_q, r_k, r_v, pos, ret_gn_w, ret_gn_b) * jax.nn.silu(z_ret)
    gates = jax.nn.sigmoid(gate_logits.astype(jnp.float32)).astype(h.dtype).reshape(B, L, N_BRANCH, D)
    merged = gates[:, :, 0] * (y_mla @ w_branch_mla) + gates[:, :, 1] * (y_ret @ w_branch_ret)
    return h + merged @ w_out


def setup_inputs(seed: int = 0) -> dict:
    key = jax.random.key(seed)
    ks = jax.random.split(key, 16)
    f32 = jnp.float32

    def w(k, shape, fan_in):
        return jax.random.normal(k, shape, f32) * (fan_in ** -0.5)

    def gain(k, shape):
        return 1.0 + 0.02 * jax.random.normal(k, shape, f32)

    return {
        "x": jax.random.normal(ks[0], (BATCH, SEQ, D_MODEL), f32),
        "meta": jax.random.normal(ks[1], (N_META, D_MODEL), f32),
        "norm_w": gain(ks[2], (DEPTH, D_MODEL)),
        "w_in": w(ks[3], (DEPTH, D_MODEL, IN_WIDTH), D_MODEL),
        "mla_q_norm_w": gain(ks[4], (DEPTH, MLA_Q_RANK)),
        "mla_w_uq": w(ks[5], (DEPTH, MLA_Q_RANK, MLA_HEADS * (MLA_NOPE + MLA_ROPE)), MLA_Q_RANK),
        "mla_kv_norm_w": gain(ks[6], (DEPTH, MLA_KV_RANK)),
        "mla_w_ukv": w(ks[7], (DEPTH, MLA_KV_RANK, MLA_HEADS * (MLA_NOPE + MLA_V)), MLA_KV_RANK),
        "ret_gn_w": gain(ks[8], (DEPTH, RET_WIDTH)),
        "ret_gn_b": 0.02 * jax.random.normal(ks[9], (DEPTH, RET_WIDTH), f32),
        "w_branch_mla": w(ks[10], (DEPTH, MLA_WIDTH, D_MODEL), MLA_WIDTH),
        "w_branch_ret": w(ks[11], (DEPTH, RET_WIDTH, D_MODEL), RET_WIDTH),
        "w_out": w(ks[12], (DEPTH, D_MODEL, D_MODEL), D_MODEL),
        "final_norm_w": gain(ks[13], (D_MODEL,)),
    }


def reference(x, meta, norm_w, w_in, mla_q_norm_w, mla_w_uq, mla_kv_norm_w, mla_w_ukv,
              ret_gn_w, ret_gn_b, w_branch_mla, w_branch_ret, w_out, final_norm_w):
    B = x.shape[0]
    h = jnp.concatenate([jnp.broadcast_to(meta.astype(x.dtype)[None], (B, N_META, D_MODEL)), x], axis=1)
    pos = jnp.arange(h.shape[1])
    for l in range(DEPTH):
        h = hybrid_layer(h, pos, norm_w[l], w_in[l], mla_q_norm_w[l], mla_w_uq[l],
                         mla_kv_norm_w[l], mla_w_ukv[l], ret_gn_w[l], ret_gn_b[l],
                         w_branch_mla[l], w_branch_ret[l], w_out[l])
    h = rmsnorm(h, final_norm_w)
    return h[:, N_META:]
```

```python
import math
from contextlib import ExitStack

import numpy as np
import concourse.bass as bass
import concourse.mybir as mybir
from concourse.bass_utils import run_bass_kernel_spmd

F32 = mybir.dt.float32
BF16 = mybir.dt.bfloat16
AF = mybir.ActivationFunctionType
ALU = mybir.AluOpType
AX = mybir.AxisListType

SEQ = 4096
D = 2048
NT = 8
T = 512
NKEY = 16 + SEQ
NBLK = 33
NU = 59
UE = 4096
SC = (128 + 64) ** -0.5
NORM_EPS = 1e-6
GN_EPS = 1e-5
U_CKV, U_KPE, U_UKV = 0, 1, 2
U_RQ, U_RK, U_RV, U_ZR = 3, 7, 11, 15
U_ZM, U_CQ, U_UQ = 19, 23, 25
U_G = 27
U_WO = 51


class _Stop(Exception):
    pass


class Buf:
    __slots__ = ("name", "w", "rs", "excl")

    def __init__(self, name, excl=False):
        self.name = name
        self.w = None
        self.rs = {}
        self.excl = excl


class Prog:
    CE = ("pe", "act", "dve", "pool")
    ALL = ("pe", "act", "dve", "pool", "sp")

    def __init__(self, K=8):
        self.ops = {e: [] for e in self.ALL}
        self.cnt = {e: 0 for e in self.CE}
        self.seen = {e: {} for e in self.ALL}
        self.K = K
        self.dcnt = {e: 0 for e in self.ALL}
        self.semkeys = set()
        self.nops = 0
        self.stop_at = None

    def op(self, eng, fn, reads=(), writes=(), dma=False):
        self.nops += 1
        if self.stop_at is not None and self.nops > self.stop_at:
            raise _Stop()
        deps = {}

        def add(ev):
            if ev is None:
                return
            sk, v = ev
            if deps.get(sk, 0) < v:
                deps[sk] = v

        for b in reads:
            add(b.w)
            if b.excl:
                for sk, v in b.rs.items():
                    if sk != eng:
                        add((sk, v))
        for b in writes:
            add(b.w)
            for sk, v in b.rs.items():
                add((sk, v))
        if dma:
            idx = self.dcnt[eng]
            self.dcnt[eng] += 1
            s = idx % self.K
            sk = ("dma", eng, s)
            val = 16 * (idx // self.K + 1)
            if val > 16:
                add((sk, val - 16))
            ev = (sk, val)
        else:
            self.cnt[eng] += 1
            ev = (eng, self.cnt[eng])
        self.semkeys.add(ev[0])
        waits = []
        seen = self.seen[eng]
        for sk, v in deps.items():
            if sk == "pe" and eng == "pe":
                continue
            if seen.get(sk, 0) < v:
                seen[sk] = v
                waits.append((sk, v))
        self.ops[eng].append((fn, waits, ev))
        for b in writes:
            b.w = ev
            b.rs = {}
        for b in reads:
            if b.rs.get(ev[0], 0) < ev[1]:
                b.rs[ev[0]] = ev[1]
        return ev

    def finish(self, eng="sp"):
        waits = []
        for e in self.ALL:
            n = self.dcnt[e]
            for s in range(min(n, self.K)):
                cnt_s = (n - 1 - s) // self.K + 1
                waits.append((("dma", e, s), 16 * cnt_s))
        self.ops[eng].append((None, waits, None))

    def emit(self, nc, sems):
        engs = {"pe": "tensor", "act": "scalar", "dve": "vector", "pool": "gpsimd", "sp": "sync"}
        with nc.Block() as block:
            for e in self.ALL:
                ops = self.ops[e]

                def body(eng, ops=ops):
                    for fn, waits, ev in ops:
                        for sk, v in waits:
                            eng.wait_ge(sems[sk], v)
                        if fn is None:
                            continue
                        ins = fn(eng)
                        ins.then_inc(sems[ev[0]], 16 if ev[0][0] == "dma" else 1)

                getattr(block, engs[e])(body)


def seq(fns):
    def f(e):
        r = None
        for g in fns:
            r = g(e)
        return r
    return f


def MM(out, lhsT, rhs, start, stop):
    return lambda e: e.matmul(out, lhsT=lhsT, rhs=rhs, start=start, stop=stop)


def TR(out, in_, ident):
    return lambda e: e.transpose(out, in_, ident)


def ACT(out, in_, func, **kw):
    return lambda e: e.activation(out=out, in_=in_, func=func, **kw)


def TT(out, in0, in1, op):
    return lambda e: e.tensor_tensor(out=out, in0=in0, in1=in1, op=op)


def TS(out, in0, s1, op0, s2=None, op1=None):
    if op1 is None:
        return lambda e: e.tensor_scalar(out=out, in0=in0, scalar1=s1, scalar2=None, op0=op0)
    return lambda e: e.tensor_scalar(out=out, in0=in0, scalar1=s1, scalar2=s2, op0=op0, op1=op1)


def STT(out, in0, scalar, in1, op0, op1):
    return lambda e: e.scalar_tensor_tensor(out=out, in0=in0, scalar=scalar, in1=in1, op0=op0, op1=op1)


def CP(out, in_):
    return lambda e: e.tensor_copy(out=out, in_=in_)


def RCP(out, in_):
    return lambda e: e.reciprocal(out=out, in_=in_)


def RED(out, in_):
    return lambda e: e.tensor_reduce(out=out, in_=in_, axis=AX.X, op=ALU.add)


def DMA(out, in_):
    return lambda e: e.dma_start(out=out, in_=in_)


def MSET(ap, v):
    return lambda e: e.memset(ap, v)


def build_nc(nt=NT, stop=None):
    nc = bass.Bass("TRN2", target_bir_lowering=False)
    x_d = nc.dram_tensor("x", [SEQ, D], F32, kind="ExternalInput").ap()
    meta_d = nc.dram_tensor("meta", [16, D], F32, kind="ExternalInput").ap()
    wpack_d = nc.dram_tensor("wpack", [NU, 128, UE], F32, kind="ExternalInput").ap()
    bc_d = nc.dram_tensor("bcv", [2, 128, D], F32, kind="ExternalInput").ap()
    cv_d = nc.dram_tensor("colv", [128, 64], F32, kind="ExternalInput").ap()
    sq_d = nc.dram_tensor("sqc", [2, 128, 128], F32, kind="ExternalInput").ap()
    tabs_d = nc.dram_tensor("tabs", [NT + 1, 128, 4, T], F32, kind="ExternalInput").ap()
    y_d = nc.dram_tensor("y", [SEQ, D], F32, kind="ExternalOutput").ap()
    wbf_d = nc.dram_tensor("wbf", [NU, 128, UE], BF16, kind="Internal").ap()
    kc_d = nc.dram_tensor("kcache", [8, 128, NKEY], BF16, kind="Internal").ap()
    vc_d = nc.dram_tensor("vcache", [8, 128, NBLK, 128], BF16, kind="Internal").ap()

    P = Prog()
    with ExitStack() as st:
        def sb(name, shape, dt):
            return st.enter_context(nc.sbuf_tensor(name, shape, dt))

        def ps(name, shape, dt=F32):
            return st.enter_context(nc.psum_tensor(name, shape, dt))

        ring = sb("ring", [128, 3, UE], BF16)
        xnT = sb("xnT", [128, 16, T], BF16)
        kpe = sb("kpe", [128, NKEY], BF16)
        tabs = sb("tabs_sb", [128, 4, T], F32)
        cqsq = sb("cqsq", [128, 2, T], F32)
        bcv = sb("bcv_sb", [128, 2, D], F32)
        colv = sb("colv_sb", [128, 64], F32)
        sqc = sb("sqc_sb", [128, 2, 128], F32)
        ident = sb("ident", [128, 128], BF16)
        tri = sb("tri", [128, 128], BF16)
        onesbf = sb("onesbf", [128, 128], BF16)
        epsv = sb("epsv", [128, 4], F32)
        xst = sb("xst", [128, 2, D], F32)
        xs = sb("xs", [128, D], BF16)
        st1 = sb("st1", [128, 8], F32)
        cqw = sb("cqw", [128, 4, T], BF16)
        ckvw = sb("ckvw", [128, 2, T], BF16)
        sq32 = sb("sq32", [128, T], F32)
        sqhl = sb("sqhl", [128, 2, 2, T], BF16)
        xin = sb("xin", [128, D], F32)
        rstd_q = sb("rstd_q", [128, T], F32)
        rstd_kv = sb("rstd_kv", [128, T], F32)
        zsm = sb("zsm", [128, 8, T], BF16)
        zsr = sb("zsr", [128, 8, T], BF16)
        qrv = sb("qrv", [128, 8, T], BF16)
        qpe = sb("qpe", [128, 4, T], BF16)
        rqkm = sb("rqkm", [128, 16, T], BF16)
        kn_st = sb("kn_st", [128, 2, T], BF16)
        v_st = sb("v_st", [128, 2, 1024], BF16)
        tmp = sb("tmp", [128, 4, T], F32)
        kbuf = sb("kbuf", [128, 2, 1024], BF16)
        vbuf = sb("vbuf", [128, 2, 8, 128], BF16)
        pT = sb("pT", [128, 3, T], BF16)
        k_tok = sb("k_tok", [128, 1024], BF16)
        v_tok = sb("v_tok", [128, 1024], BF16)
        vz_tok = sb("vz_tok", [128, 1024], BF16)
        SmT = sb("SmT", [128, 8, 128], BF16)
        o_n = sb("o_n", [128, 1024], BF16)
        R32 = sb("R32", [128, 8, 128], F32)
        Rbf = sb("Rbf", [128, 8, 128], BF16)
        gst = sb("gst", [128, 8, 8], F32)

        pb = [ps(f"pb{i}", [128, 512]) for i in range(6)]
        pt6 = ps("pt6", [128, 1024], BF16)
        pt7 = ps("pt7", [128, 1024], BF16)

        B = {}

        def bufs(name, n=None, excl=False):
            if n is None:
                B[name] = Buf(name, excl)
            else:
                B[name] = [Buf(f"{name}{i}", excl) for i in range(n)]
            return B[name]

        b_ring = bufs("ring", 3)
        b_wbf = bufs("wbf", NU)
        b_xnT = bufs("xnT", 4)
        b_kpe = bufs("kpe")
        b_tabs = bufs("tabs")
        b_cqsq = bufs("cqsq")
        b_const = bufs("const")
        b_xst = bufs("xst", 2)
        b_xs = bufs("xs")
        b_st1 = bufs("st1")
        b_st2 = bufs("st2")
        b_cqw = bufs("cqw", 4)
        b_ckvw = bufs("ckvw", 2)
        b_sq32 = bufs("sq32")
        b_sqhl = bufs("sqhl", 2)
        b_xin = bufs("xin")
        b_rq = bufs("rstd_q")
        b_rkv = bufs("rstd_kv")
        b_zsm = bufs("zsm", 8)
        b_zsr = bufs("zsr", 8)
        b_qrv = bufs("qrv", 8)
        b_qpe = bufs("qpe", 4)
        b_rqkm = bufs("rqkm", 16)
        b_knst = bufs("knst", 2)
        b_vst = bufs("vst", 2)
        b_tmp = bufs("tmp", 4)
        b_kbuf = bufs("kbuf", 2)
        b_vbuf = bufs("vbuf", 2)
        b_pT = bufs("pT", 3)
        b_ktok = bufs("ktok")
        b_vtok = bufs("vtok")
        b_vz = bufs("vz")
        b_SmT = bufs("SmT", 8)
        b_on = bufs("on", 8)
        b_R32 = bufs("R32", 8)
        b_Rbf = bufs("Rbf")
        b_gst = bufs("gst", 8)
        b_pb = bufs("pb", 6, excl=True)
        b_pt6 = bufs("pt6", excl=True)
        b_pt7 = bufs("pt7", excl=True)
        b_kc = bufs("kc", 8)
        b_vc = bufs("vc", 8)
        b_y = bufs("y")

        P.op("sp", DMA(bcv[:], bc_d.rearrange("k p n -> p k n")), writes=[b_const], dma=True)
        P.op("sp", DMA(colv[:], cv_d), writes=[b_const], dma=True)
        P.op("sp", DMA(sqc[:], sq_d.rearrange("k p n -> p k n")), writes=[b_const], dma=True)
        P.op("dve", CP(ident[:], sqc[:, 0, :]), reads=[b_const], writes=[b_const])
        P.op("dve", CP(tri[:], sqc[:, 1, :]), reads=[b_const], writes=[b_const])
        P.op("pool", MSET(onesbf[:], 1.0), writes=[b_const])
        P.op("pool", MSET(epsv[:, 0:1], NORM_EPS), writes=[b_const])
        P.op("pool", MSET(epsv[:, 1:2], NORM_EPS / (SC * SC)), writes=[b_const])
        P.op("pool", MSET(R32[:], 0.0), writes=b_R32)
        P.op("pool", MSET(Rbf[:], 0.0), writes=[b_Rbf])
        nw_bc = bcv[:, 0, :]
        fnw_bc = bcv[:, 1, :]
        QNW, KVNW, GNW, GNB, CDEC, CZETA, EPSX, CZM = 0, 4, 8, 16, 24, 32, 40, 48

        def cvc(base, j, rows=128):
            return colv[0:rows, base + j:base + j + 1]

        log_g = [math.log1p(-(2.0 ** (-5.0 - h))) for h in range(8)]
        g128 = [math.exp(lg * 128.0) for lg in log_g]

        items = [U_CKV, U_KPE, U_UKV] + list(range(U_RK, U_RK + 4)) + list(range(U_RV, U_RV + 4))
        for _t in range(nt):
            items += list(range(0, U_WO)) + list(range(U_WO, U_WO + 8)) + list(range(U_WO, U_WO + 8))
        S = {"pos": 0, "loaded": 0, "casted": set(), "castpos": 0}

        def stream_next(expect, prefetch=2):
            i = S["pos"]
            assert items[i] == expect, (i, items[i], expect)
            while S["castpos"] < min(len(items), i + 14):
                u = items[S["castpos"]]
                if u not in S["casted"]:
                    S["casted"].add(u)
                    P.op("pool", DMA(wbf_d[u], wpack_d[u]), writes=[b_wbf[u]], dma=True)
                S["castpos"] += 1
            while S["loaded"] < min(len(items), i + prefetch + 1):
                k = S["loaded"]
                u = items[k]
                P.op("sp", DMA(ring[:, k % 3, :], wbf_d[u]), reads=[b_wbf[u]], writes=[b_ring[k % 3]], dma=True)
                S["loaded"] += 1
            S["pos"] += 1
            return i % 3

        acc_i = [0]

        def next_acc():
            acc_i[0] ^= 1
            return acc_i[0]

        def s1_pre(t, cc):
            meta = t < 0
            rows = 16 if meta else 128
            src = meta_d if meta else x_d[T * t + 128 * cc:T * t + 128 * cc + 128, :]
            P.op("sp", DMA(xin[0:rows, :], src), writes=[b_xin], dma=True)
            P.op("pool", MSET(st1[0:rows, 0:1], 0.0), writes=[b_st1])
            P.op("act", ACT(xs[0:rows, :], xin[0:rows, :], AF.Square, accum_out=st1[0:rows, 0:1]),
                 reads=[b_xin], writes=[b_xs, b_st1])
            P.op("act", ACT(st1[0:rows, 1:2], st1[0:rows, 0:1], AF.Sqrt, bias=epsv[0:rows, 0:1], scale=1.0 / D),
                 reads=[b_st1, b_const], writes=[b_st1])
            P.op("dve", RCP(st1[0:rows, 2:3], st1[0:rows, 1:2]), reads=[b_st1], writes=[b_st1])
            P.op("dve", STT(xs[0:rows, :], xin[0:rows, :], st1[0:rows, 2:3], nw_bc[0:rows, :], ALU.mult, ALU.mult),
                 reads=[b_xin, b_st1, b_const], writes=[b_xs])

        def s1_post(t, cc):
            meta = t < 0
            rows = 16 if meta else 128
            P.op("pe", seq([TR(pt6[:, k * 128:k * 128 + rows], xs[0:rows, k * 128:(k + 1) * 128], ident[0:rows, 0:rows])
                            for k in range(8)]), reads=[b_xs, b_const], writes=[b_pt6])
            P.op("pe", seq([TR(pt7[:, k * 128:k * 128 + rows], xs[0:rows, (8 + k) * 128:(9 + k) * 128], ident[0:rows, 0:rows])
                            for k in range(8)]), reads=[b_xs, b_const], writes=[b_pt7])
            P.op("act", ACT(xnT[:, 0:8, cc * 128:cc * 128 + rows],
                            pt6[:].rearrange("p (k n) -> p k n", k=8)[:, :, 0:rows], AF.Copy),
                 reads=[b_pt6], writes=[b_xnT[cc]])
            P.op("dve", CP(xnT[:, 8:16, cc * 128:cc * 128 + rows],
                           pt7[:].rearrange("p (k n) -> p k n", k=8)[:, :, 0:rows]),
                 reads=[b_pt7], writes=[b_xnT[cc]])

        def tile(t):
            if stop is not None and stop[0] == t and stop[1].startswith("op"):
                P.nops = 0
                P.stop_at = int(stop[1][2:])
            meta = t < 0
            N = 16 if meta else T
            nch = 1 if meta else 4
            rows = 16 if meta else 128
            key0 = 0 if meta else 16 + T * t
            ti = 0 if meta else t + 1

            P.op("sp", DMA(tabs[:], tabs_d[ti]), writes=[b_tabs], dma=True)
            cosR, sinR, Cm, Sm = tabs[:, 0, 0:N], tabs[:, 1, 0:N], tabs[:, 2, 0:N], tabs[:, 3, 0:N]

            if meta or t == 0:
                for cc in range(nch):
                    s1_pre(t, cc)
                    s1_post(t, cc)

            if stop == (t, 's1'):
                raise _Stop()
            def inproj(slot, blk, a):
                base = blk * 16 * 128
                P.op("pe", seq([MM(pb[a][:, 0:N], ring[:, slot, base + kc * 128:base + (kc + 1) * 128], xnT[:, kc, 0:N],
                                   kc == 0, kc == 15) for kc in range(16)]),
                     reads=[b_ring[slot]] + b_xnT[0:nch], writes=[b_pb[a]])

            def sumsq_block(a, par, n, first, last):
                P.op("act", ACT(sq32[:, 0:n], pb[a][:, 0:n], AF.Square), reads=[b_pb[a]], writes=[b_sq32])
                P.op("dve", CP(sqhl[:, par, 0, 0:n], sq32[:, 0:n]), reads=[b_sq32], writes=[b_sqhl[par]])
                P.op("pool", TT(sqhl[:, par, 1, 0:n], sq32[:, 0:n], sqhl[:, par, 0, 0:n], ALU.subtract),
                     reads=[b_sq32, b_sqhl[par]], writes=[b_sqhl[par]])
                P.op("pe", seq([MM(pb[4][:, 0:n], onesbf[:], sqhl[:, par, 0, 0:n], first, False),
                                MM(pb[4][:, 0:n], onesbf[:], sqhl[:, par, 1, 0:n], False, last)]),
                     reads=[b_const, b_sqhl[par]], writes=[b_pb[4]])

            slot = stream_next(U_CKV)
            for kc in range(2):
                a = next_acc()
                inproj(slot, kc, a)
                P.op("dve", CP(tmp[:, 2 + kc, 0:N], pb[a][:, 0:N]), reads=[b_pb[a]], writes=[b_tmp[2 + kc]])
                sumsq_block(a, kc, N, kc == 0, kc == 1)
            P.op("act", ACT(rstd_kv[:, 0:N], pb[4][:, 0:N], AF.Sqrt, bias=epsv[:, 0:1], scale=1.0 / 256),
                 reads=[b_pb[4], b_const], writes=[b_rkv])
            P.op("dve", RCP(rstd_kv[:, 0:N], rstd_kv[:, 0:N]), reads=[b_rkv], writes=[b_rkv])
            for kc in range(2):
                P.op("dve", STT(ckvw[:, kc, 0:N], tmp[:, 2 + kc, 0:N], cvc(KVNW, kc), rstd_kv[:, 0:N], ALU.mult, ALU.mult),
                     reads=[b_tmp[2 + kc], b_const, b_rkv], writes=[b_ckvw[kc]])

            if stop == (t, 's2a1'):
                raise _Stop()
            slot = stream_next(U_KPE)
            inproj(slot, 0, 0)
            inproj(slot, 1, 1)
            P.op("dve", TT(tmp[:, 0, 0:N], pb[0][:, 0:N], Cm, ALU.mult), reads=[b_pb[0], b_tabs], writes=[b_tmp[0]])
            P.op("dve", TT(tmp[:, 1, 0:N], pb[1][:, 0:N], Sm, ALU.mult), reads=[b_pb[1], b_tabs], writes=[b_tmp[1]])
            P.op("pool", TT(kpe[:, key0:key0 + N], tmp[:, 0, 0:N], tmp[:, 1, 0:N], ALU.add),
                 reads=[b_tmp[0], b_tmp[1]], writes=[b_kpe])

            if stop == (t, 's2a2'):
                raise _Stop()
            slot = stream_next(U_UKV)
            for h in range(8):
                a = next_acc()
                base = h * 2 * 128
                P.op("pe", seq([MM(pb[a][:, 0:N], ring[:, slot, base + kc * 128:base + (kc + 1) * 128], ckvw[:, kc, 0:N],
                                   kc == 0, kc == 1) for kc in range(2)]),
                     reads=[b_ring[slot]] + b_ckvw, writes=[b_pb[a]])
                P.op("dve", CP(kn_st[:, h % 2, 0:N], pb[a][:, 0:N]), reads=[b_pb[a]], writes=[b_knst[h % 2]])
                P.op("sp", DMA(kc_d[h, :, key0:key0 + N], kn_st[:, h % 2, 0:N]), reads=[b_knst[h % 2]], writes=[b_kc[h]], dma=True)
            if stop == (t, 's2a3'):
                raise _Stop()
            ringv = ring[:, slot, :].rearrange("p (b k c) -> p b k c", b=16, k=2)
            for cc in range(nch):
                for half in range(2):
                    a = 4 + half
                    P.op("pe", seq([MM(pb[a][0:rows, :], ckvw[:, kc, cc * 128:cc * 128 + rows],
                                       ringv[:, 8 + 4 * half:12 + 4 * half, kc, :], kc == 0, kc == 1) for kc in range(2)]),
                         reads=[b_ring[slot]] + b_ckvw, writes=[b_pb[a]])
                    P.op("act", ACT(v_st[0:rows, cc % 2, half * 512:(half + 1) * 512], pb[a][0:rows, :], AF.Copy),
                         reads=[b_pb[a]], writes=[b_vst[cc % 2]])
                blk = 0 if meta else 1 + 4 * t + cc
                for h in range(8):
                    P.op("sp", DMA(vc_d[h, 0:rows, blk, :], v_st[0:rows, cc % 2, h * 128:(h + 1) * 128]),
                         reads=[b_vst[cc % 2]], writes=[b_vc[h]], dma=True)

            if stop == (t, 's2a'):
                raise _Stop()
            rp_i = [0]

            def rope_pair(slot, dst0):
                bA = 2 * (rp_i[0] % 2)
                bB = bA + 1
                rp_i[0] += 1
                inproj(slot, 0, bA)
                inproj(slot, 1, bB)
                P.op("dve", TT(tmp[:, 0, 0:N], pb[bA][:, 0:N], cosR, ALU.mult), reads=[b_pb[bA], b_tabs], writes=[b_tmp[0]])
                P.op("dve", TT(tmp[:, 1, 0:N], pb[bB][:, 0:N], sinR, ALU.mult), reads=[b_pb[bB], b_tabs], writes=[b_tmp[1]])
                P.op("pool", TT(rqkm[:, dst0, 0:N], tmp[:, 0, 0:N], tmp[:, 1, 0:N], ALU.subtract),
                     reads=[b_tmp[0], b_tmp[1]], writes=[b_rqkm[dst0]])
                P.op("dve", TT(tmp[:, 2, 0:N], pb[bB][:, 0:N], cosR, ALU.mult), reads=[b_pb[bB], b_tabs], writes=[b_tmp[2]])
                P.op("dve", TT(tmp[:, 3, 0:N], pb[bA][:, 0:N], sinR, ALU.mult), reads=[b_pb[bA], b_tabs], writes=[b_tmp[3]])
                P.op("pool", TT(rqkm[:, dst0 + 1, 0:N], tmp[:, 2, 0:N], tmp[:, 3, 0:N], ALU.add),
                     reads=[b_tmp[2], b_tmp[3]], writes=[b_rqkm[dst0 + 1]])

            if not meta:
                for p in range(4):
                    rope_pair(stream_next(U_RQ + p), 2 * p)
            for p in range(4):
                rope_pair(stream_next(U_RK + p), 8 + 2 * p)
            for u in range(4):
                slot = stream_next(U_RV + u)
                for bb in range(2):
                    h = 2 * u + bb
                    a = next_acc()
                    inproj(slot, bb, a)
                    P.op("act", ACT(qrv[:, h, 0:N], pb[a][:, 0:N], AF.Copy), reads=[b_pb[a]], writes=[b_qrv[h]])
            def silu_unit(ubase, u, dst, dbufs):
                slot = stream_next(ubase + u)
                for bb in range(2):
                    h = 2 * u + bb
                    a = next_acc()
                    inproj(slot, bb, a)
                    P.op("act", ACT(dst[:, h, :], pb[a][:, :], AF.Silu), reads=[b_pb[a]], writes=[dbufs[h]])

            fillers = []
            if not meta:
                silu_unit(U_ZR, 0, zsr, b_zsr)
                silu_unit(U_ZR, 1, zsr, b_zsr)
                fillers = [lambda: silu_unit(U_ZR, 2, zsr, b_zsr), lambda: silu_unit(U_ZR, 3, zsr, b_zsr)] + \
                          [(lambda u=u: silu_unit(U_ZM, u, zsm, b_zsm)) for u in range(4)]

            def fill():
                if fillers:
                    fillers.pop(0)()

            if stop == (t, 'Rp'):
                raise _Stop()
            for cc in range(nch):
                s0 = cc * 128
                P.op("pe", seq([TR(pt6[0:rows, j * 128:(j + 1) * 128], rqkm[:, 8 + j, s0:s0 + rows], ident[:]) for j in range(8)]),
                     reads=b_rqkm[8:16] + [b_const], writes=[b_pt6])
                P.op("pe", seq([TR(pt7[0:rows, h * 128:(h + 1) * 128], qrv[:, h, s0:s0 + rows], ident[:]) for h in range(8)]),
                     reads=b_qrv + [b_const], writes=[b_pt7])
                P.op("act", ACT(k_tok[0:rows, :], pt6[0:rows, :], AF.Copy), reads=[b_pt6], writes=[b_ktok])
                zb = CZM if meta else CZETA
                for h in range(8):
                    P.op("act", ACT(vz_tok[0:rows, h * 128:(h + 1) * 128], pt7[0:rows, h * 128:(h + 1) * 128], AF.Identity,
                                    scale=cvc(zb, h, rows)), reads=[b_pt7, b_const], writes=[b_vz])
                if not meta:
                    P.op("dve", CP(v_tok[:], pt7[:]), reads=[b_pt7], writes=[b_vtok])
                    for hh in range(2):
                        r0, r1 = hh * 64, hh * 64 + 64
                        fns = []
                        for p in range(4):
                            o = pb[2 + hh][:, p * 128:(p + 1) * 128]
                            fns.append(MM(o, rqkm[r0:r1, 8 + 2 * p, s0:s0 + 128], rqkm[r0:r1, 2 * p, s0:s0 + 128], True, False))
                            fns.append(MM(o, rqkm[r0:r1, 9 + 2 * p, s0:s0 + 128], rqkm[r0:r1, 2 * p + 1, s0:s0 + 128], False, True))
                        P.op("pe", seq(fns), reads=b_rqkm, writes=[b_pb[2 + hh]])
                    for h in range(8):
                        hh, p = h % 2, h // 2
                        P.op("dve", STT(SmT[:, h, :], pb[2 + hh][:, p * 128:(p + 1) * 128], cvc(CDEC, h), tri[:],
                                        ALU.mult, ALU.mult), reads=[b_pb[2 + hh], b_const], writes=[b_SmT[h]])
                    fill()
                    for hh in range(2):
                        r0, r1 = hh * 64, hh * 64 + 64
                        fns = []
                        for p in range(4):
                            h = 2 * p + hh
                            o = pb[4 + hh][:, p * 128:(p + 1) * 128]
                            fns.append(MM(o, SmT[:, h, :], v_tok[:, h * 128:(h + 1) * 128], True, False))
                            fns.append(MM(o, rqkm[r0:r1, 2 * p, s0:s0 + 128], Rbf[r0:r1, 2 * p, :], False, False))
                            fns.append(MM(o, rqkm[r0:r1, 2 * p + 1, s0:s0 + 128], Rbf[r0:r1, 2 * p + 1, :], False, True))
                        P.op("pe", seq(fns), reads=[b_SmT[2 * p + hh] for p in range(4)] + [b_vtok, b_Rbf] + b_rqkm[0:8],
                             writes=[b_pb[4 + hh]])
                    sqo = tmp[:, 0:2, :]
                    for hh in range(2):
                        P.op("dve", RED(gst[:, 0, 4 * hh:4 * hh + 4], pb[4 + hh][:].rearrange("p (h e) -> p h e", h=4)),
                             reads=[b_pb[4 + hh]], writes=[b_gst[0]])
                        P.op("act", ACT(tmp[:, hh, :], pb[4 + hh][:], AF.Square), reads=[b_pb[4 + hh]], writes=[b_tmp[hh]])
                    P.op("dve", RED(gst[:, 1, :], sqo.rearrange("p a (h e) -> p (a h) e", h=4)), reads=b_tmp[0:2], writes=[b_gst[1]])
                    P.op("dve", TS(gst[:, 2, :], gst[:, 0, :], 1.0 / 128, ALU.mult), reads=[b_gst[0]], writes=[b_gst[2]])
                    P.op("dve", TT(gst[:, 3, :], gst[:, 2, :], gst[:, 2, :], ALU.mult), reads=[b_gst[2]], writes=[b_gst[3]])
                    P.op("dve", STT(gst[:, 4, :], gst[:, 1, :], 1.0 / 128, gst[:, 3, :], ALU.mult, ALU.subtract),
                         reads=[b_gst[1], b_gst[3]], writes=[b_gst[4]])
                    P.op("dve", TT(gst[:, 4, :], gst[:, 4, :], colv[:, EPSX:EPSX + 8], ALU.add), reads=[b_gst[4], b_const], writes=[b_gst[4]])
                    P.op("act", ACT(gst[:, 5, :], gst[:, 4, :], AF.Sqrt), reads=[b_gst[4]], writes=[b_gst[5]])
                    P.op("dve", RCP(gst[:, 6, :], gst[:, 5, :]), reads=[b_gst[5]], writes=[b_gst[6]])
                    P.op("dve", STT(gst[:, 7, :], gst[:, 2, :], -1.0, gst[:, 6, :], ALU.mult, ALU.mult),
                         reads=[b_gst[2], b_gst[6]], writes=[b_gst[7]])
                    for h in range(8):
                        hh, p = h % 2, h // 2
                        i = hh * 4 + p
                        P.op("act", ACT(o_n[:, h * 128:(h + 1) * 128], pb[4 + hh][:, p * 128:(p + 1) * 128], AF.Identity,
                                        scale=gst[:, 6, i:i + 1], bias=gst[:, 7, i:i + 1]),
                             reads=[b_pb[4 + hh], b_gst[6], b_gst[7]], writes=[b_on[h]])
                    fill()
                    P.op("pe", seq([TR(pt6[:, h * 128:(h + 1) * 128], o_n[:, h * 128:(h + 1) * 128], ident[:]) for h in range(8)]),
                         reads=b_on + [b_const], writes=[b_pt6])
                    ont = tmp[:, 2:4, :].rearrange("p a (h e) -> p (a h) e", h=4)
                    for h in range(8):
                        P.op("act", ACT(ont[:, h, :], pt6[:, h * 128:(h + 1) * 128], AF.Identity,
                                        scale=cvc(GNW, h), bias=cvc(GNB, h)),
                             reads=[b_pt6, b_const], writes=[b_tmp[2 + h // 4]])
                    P.op("pool", TT(zsr[:, :, s0:s0 + 128], ont, zsr[:, :, s0:s0 + 128], ALU.mult),
                         reads=b_tmp[2:4] + b_zsr, writes=b_zsr)
                for j in range(8):
                    p = j // 2
                    a = j // 2
                    o = pb[a][:, (j % 2) * 256:(j % 2) * 256 + 256]
                    P.op("pe", MM(o, k_tok[0:rows, j * 128:(j + 1) * 128], vz_tok[0:rows, p * 256:(p + 1) * 256], True, True),
                         reads=[b_ktok, b_vz], writes=[b_pb[a]])
                for j in range(8):
                    p = j // 2
                    a = j // 2
                    for hh in range(2):
                        h = 2 * p + hh
                        r0, r1 = hh * 64, hh * 64 + 64
                        src = pb[a][r0:r1, (j % 2) * 256 + hh * 128:(j % 2) * 256 + hh * 128 + 128]
                        P.op("dve", STT(R32[r0:r1, j, :], R32[r0:r1, j, :], g128[h], src, ALU.mult, ALU.add),
                             reads=[b_pb[a], b_R32[j]], writes=[b_R32[j]])
                P.op("pool", CP(Rbf[:], R32[:]), reads=b_R32, writes=[b_Rbf])
            if stop == (t, 'R'):
                raise _Stop()
            while fillers:
                fill()
            if meta:
                return

            for u in range(2):
                slot = stream_next(U_CQ + u)
                for bb in range(2):
                    j = 2 * u + bb
                    a = next_acc()
                    inproj(slot, bb, a)
                    P.op("dve", TS(cqw[:, j, :], pb[a][:], cvc(QNW, j), ALU.mult), reads=[b_pb[a], b_const], writes=[b_cqw[j]])
                    sumsq_block(a, j % 2, T, j == 0, j == 3)
            P.op("act", ACT(rstd_q[:], pb[4][:], AF.Sqrt, bias=epsv[:, 1:2], scale=1.0 / (512 * SC * SC)),
                 reads=[b_pb[4], b_const], writes=[b_rq])
            P.op("dve", RCP(rstd_q[:], rstd_q[:]), reads=[b_rq], writes=[b_rq])
            P.op("pool", TT(cqsq[:, 0, :], tabs[:, 2, :], rstd_q[:], ALU.mult), reads=[b_tabs, b_rq], writes=[b_cqsq])
            P.op("pool", TT(cqsq[:, 1, :], tabs[:, 3, :], rstd_q[:], ALU.mult), reads=[b_tabs, b_rq], writes=[b_cqsq])

            def uqproj(slot, blk, a):
                base = blk * 4 * 128
                P.op("pe", seq([MM(pb[a][:], ring[:, slot, base + kc * 128:base + (kc + 1) * 128], cqw[:, kc, :], kc == 0, kc == 3)
                                for kc in range(4)]), reads=[b_ring[slot]] + b_cqw, writes=[b_pb[a]])

            slot = stream_next(U_UQ)
            for h in range(8):
                a = next_acc()
                uqproj(slot, h, a)
                P.op("dve", TT(qrv[:, h, :], pb[a][:], rstd_q[:], ALU.mult), reads=[b_pb[a], b_rq], writes=[b_qrv[h]])
            slot = stream_next(U_UQ + 1)
            for p in range(4):
                uqproj(slot, 2 * p, 0)
                uqproj(slot, 2 * p + 1, 1)
                P.op("dve", TT(tmp[:, 0, :], pb[0][:], cqsq[:, 0, :], ALU.mult), reads=[b_pb[0], b_cqsq], writes=[b_tmp[0]])
                P.op("dve", TT(tmp[:, 1, :], pb[1][:], cqsq[:, 1, :], ALU.mult), reads=[b_pb[1], b_cqsq], writes=[b_tmp[1]])
                P.op("pool", TT(qpe[:, p, :], tmp[:, 0, :], tmp[:, 1, :], ALU.add), reads=b_tmp[0:2], writes=[b_qpe[p]])

            if stop == (t, 's2b'):
                raise _Stop()
            nblk = 4 * t + 5
            units = []
            gi = [0]
            for h in range(8):
                for g in range((nblk + 7) // 8):
                    b0, b1 = 8 * g, min(8 * g + 8, nblk)
                    c0 = 0 if g == 0 else 16 + 128 * (b0 - 1)
                    c1 = 16 + 128 * (b1 - 1)
                    for j in range(b0, b1):
                        units.append((h, g, j, b0, b1, c0, c1))
            loaded = {}

            def load_group(h, g, b0, b1, c0, c1):
                kb_i = gi[0] % 2
                gi[0] += 1
                P.op("sp", DMA(kbuf[:, kb_i, 0:c1 - c0], kc_d[h, :, c0:c1]), reads=[b_kc[h]], writes=[b_kbuf[kb_i]], dma=True)
                if b0 == 0:
                    P.op("sp", DMA(vbuf[0:16, kb_i, 0, :], vc_d[h, 0:16, 0, :]), reads=[b_vc[h]], writes=[b_vbuf[kb_i]], dma=True)
                    P.op("sp", DMA(vbuf[:, kb_i, 1:b1, :], vc_d[h, :, 1:b1, :]), reads=[b_vc[h]], writes=[b_vbuf[kb_i]], dma=True)
                else:
                    P.op("sp", DMA(vbuf[:, kb_i, 0:b1 - b0, :], vc_d[h, :, b0:b1, :]), reads=[b_vc[h]], writes=[b_vbuf[kb_i]], dma=True)
                loaded[(h, g)] = kb_i

            def emit_S(ui):
                h, g, j, b0, b1, c0, c1 = units[ui]
                if (h, g) not in loaded:
                    load_group(h, g, b0, b1, c0, c1)
                kb_i = loaded[(h, g)]
                nk = 16 if j == 0 else 128
                kc0 = 0 if j == 0 else 16 + 128 * (j - 1) - c0
                kabs = 0 if j == 0 else 16 + 128 * (j - 1)
                kb = j - (4 * t + 1)
                q0 = 128 * kb if kb >= 0 else 0
                sb_i = ui % 4
                hh = h % 2
                r0, r1 = hh * 64, hh * 64 + 64
                P.op("pe", seq([
                    MM(pb[sb_i][0:nk, q0:T], kbuf[:, kb_i, kc0:kc0 + nk], qrv[:, h, q0:T], True, False),
                    MM(pb[sb_i][0:nk, q0:T], kpe[r0:r1, kabs:kabs + nk], qpe[r0:r1, h // 2, q0:T], False, True)]),
                    reads=[b_kbuf[kb_i], b_qrv[h], b_kpe, b_qpe[h // 2]], writes=[b_pb[sb_i]])

            def emit_PV(ui):
                h, g, j, b0, b1, c0, c1 = units[ui]
                kb_i = loaded[(h, g)]
                nk = 16 if j == 0 else 128
                kb = j - (4 * t + 1)
                q0 = 128 * kb if kb >= 0 else 0
                sb_i = ui % 4
                pi = ui % 3
                po, psm = 4, 5
                P.op("act", ACT(pT[0:nk, pi, q0:T], pb[sb_i][0:nk, q0:T], AF.Exp), reads=[b_pb[sb_i]], writes=[b_pT[pi]])
                if kb >= 0:
                    P.op("dve", TT(pT[:, pi, q0:q0 + 128], pT[:, pi, q0:q0 + 128], tri[:], ALU.mult),
                         reads=[b_pT[pi], b_const], writes=[b_pT[pi]])
                first, last = (j == 0), (j == nblk - 1)
                P.op("pe", seq([
                    MM(pb[po][:, q0:T], vbuf[0:nk, kb_i, j - b0, :], pT[0:nk, pi, q0:T], first, last),
                    MM(pb[psm][:, q0:T], onesbf[0:nk, :], pT[0:nk, pi, q0:T], first, last)]),
                    reads=[b_vbuf[kb_i], b_pT[pi], b_const], writes=[b_pb[po], b_pb[psm]])
                if last:
                    P.op("dve", RCP(tmp[:, 0, :], pb[psm][:]), reads=[b_pb[psm]], writes=[b_tmp[0]])
                    P.op("dve", TT(tmp[:, 1, :], pb[po][:], tmp[:, 0, :], ALU.mult), reads=[b_pb[po], b_tmp[0]], writes=[b_tmp[1]])
                    P.op("pool", TT(zsm[:, h, :], tmp[:, 1, :], zsm[:, h, :], ALU.mult), reads=[b_tmp[1], b_zsm[h]], writes=[b_zsm[h]])

            DEPTH = 3
            gkeys = []
            for (h, g, j, b0, b1, c0, c1) in units:
                if not gkeys or gkeys[-1] != (h, g):
                    gkeys.append((h, g))
            gseq = {k: i for i, k in enumerate(gkeys)}
            last_unit = {}
            for ui, (h, g, j, b0, b1, c0, c1) in enumerate(units):
                last_unit[gseq[(h, g)]] = ui
            st_ = {"next_S": 0, "pv_done": 0}

            def can_emit(k):
                h, g = units[k][0], units[k][1]
                if (h, g) in loaded:
                    return True
                q = gseq[(h, g)] - 2
                return q < 0 or last_unit[q] < st_["pv_done"]

            def pump(limit):
                while st_["next_S"] < min(limit, len(units)) and can_emit(st_["next_S"]):
                    emit_S(st_["next_S"])
                    st_["next_S"] += 1

            for ui in range(len(units)):
                pump(ui + DEPTH + 1)
                assert st_["next_S"] > ui
                emit_PV(ui)
                st_["pv_done"] = ui + 1

            if stop == (t, 'A'):
                raise _Stop()
            def brproj(slot, blk, a, src, srcbufs):
                base = blk * 8 * 128
                P.op("pe", seq([MM(pb[a][:], ring[:, slot, base + kc * 128:base + (kc + 1) * 128], src[:, kc, :], kc == 0, kc == 7)
                                for kc in range(8)]), reads=[b_ring[slot]] + srcbufs, writes=[b_pb[a]])

            def merge(f):
                P.op("act", ACT(tmp[:, 0, :], pb[0][:], AF.Sigmoid), reads=[b_pb[0]], writes=[b_tmp[0]])
                P.op("act", ACT(tmp[:, 1, :], pb[1][:], AF.Sigmoid), reads=[b_pb[1]], writes=[b_tmp[1]])
                P.op("dve", TT(tmp[:, 2, :], tmp[:, 0, :], pb[2][:], ALU.mult), reads=[b_tmp[0], b_pb[2]], writes=[b_tmp[2]])
                P.op("dve", TT(tmp[:, 3, :], tmp[:, 1, :], pb[3][:], ALU.mult), reads=[b_tmp[1], b_pb[3]], writes=[b_tmp[3]])
                P.op("pool", TT(rqkm[:, f, :], tmp[:, 2, :], tmp[:, 3, :], ALU.add), reads=b_tmp[2:4], writes=[b_rqkm[f]])

            for pj in range(8):
                f = 2 * pj
                sA = stream_next(U_G + 3 * pj)
                inproj(sA, 0, 0)
                inproj(sA, 1, 1)
                sC = stream_next(U_G + 3 * pj + 1)
                brproj(sC, 0, 2, zsm, b_zsm)
                brproj(sC, 1, 3, zsr, b_zsr)
                merge(f)
                sB = stream_next(U_G + 3 * pj + 2, prefetch=1)
                inproj(sB, 0, 0)
                inproj(sB, 1, 1)
                brproj(sC, 2, 2, zsm, b_zsm)
                brproj(sC, 3, 3, zsr, b_zsr)
                merge(f + 1)

            if stop == (t, 'G'):
                raise _Stop()
            for pp in range(2):
                for c2 in range(2):
                    cc = 2 * pp + c2
                    P.op("sp", DMA(xst[:, c2, :], x_d[T * t + 128 * cc:T * t + 128 * cc + 128, :]), writes=[b_xst[c2]], dma=True)
                oi = 0
                nxt = (pp == 0 and t + 1 < nt)
                if nxt:
                    s1_pre(t + 1, 0)
                for u in range(8):
                    if nxt and u in (2, 4, 6):
                        s1_post(t + 1, u // 2 - 1)
                        s1_pre(t + 1, u // 2)
                    slot = stream_next(U_WO + u)
                    for c2 in range(2):
                        cc = 2 * pp + c2
                        a = oi % 4
                        oi += 1
                        P.op("pe", seq([MM(pb[a][:, 0:256], rqkm[:, kc, cc * 128:(cc + 1) * 128], ring[:, slot, kc * 256:(kc + 1) * 256],
                                           kc == 0, kc == 15) for kc in range(16)]),
                             reads=[b_ring[slot]] + b_rqkm, writes=[b_pb[a]])
                        P.op("dve", TT(xst[:, c2, u * 256:(u + 1) * 256], pb[a][:, 0:256], xst[:, c2, u * 256:(u + 1) * 256], ALU.add),
                             reads=[b_pb[a], b_xst[c2]], writes=[b_xst[c2]])
                if nxt:
                    s1_post(t + 1, 3)
                for c2 in range(2):
                    cc = 2 * pp + c2
                    P.op("pool", MSET(st1[:, 4:5], 0.0), writes=[b_st2])
                    P.op("act", ACT(kbuf[:].rearrange("p a n -> p (a n)"), xst[:, c2, :], AF.Square, accum_out=st1[:, 4:5]),
                         reads=[b_xst[c2]], writes=b_kbuf + [b_st2])
                    P.op("act", ACT(st1[:, 5:6], st1[:, 4:5], AF.Sqrt, bias=epsv[:, 0:1], scale=1.0 / D),
                         reads=[b_st2, b_const], writes=[b_st2])
                    P.op("dve", RCP(st1[:, 6:7], st1[:, 5:6]), reads=[b_st2], writes=[b_st2])
                    P.op("dve", STT(xst[:, c2, :], xst[:, c2, :], st1[:, 6:7], fnw_bc, ALU.mult, ALU.mult),
                         reads=[b_xst[c2], b_st2, b_const], writes=[b_xst[c2]])
                    P.op("sp", DMA(y_d[T * t + 128 * cc:T * t + 128 * cc + 128, :], xst[:, c2, :]),
                         reads=[b_xst[c2]], writes=[b_y], dma=True)

        try:
            tile(-1)
            for t in range(nt):
                tile(t)
            assert S["pos"] == len(items), (S["pos"], len(items))
        except _Stop:
            pass
        P.finish()

        sems = {}
        for sk in sorted(P.semkeys, key=str):
            nm = "s_" + "".join(ch if ch.isalnum() else "_" for ch in str(sk))
            sems[sk] = st.enter_context(nc.semaphore(nm))
        P.emit(nc, sems)
    return nc


def _std_unit(W, KC, col_lists):
    blks = []
    for cols in col_lists:
        blk = W[:, cols].reshape(KC, 128, 128).transpose(1, 0, 2)
        blks.append(blk.reshape(128, KC * 128))
    u = np.concatenate(blks, axis=1)
    assert u.shape == (128, UE), u.shape
    return u


def _build_wpack(w_in, w_uq, w_ukv, w_bm, w_br, w_out):
    ar = np.arange
    units = [None] * NU
    units[U_CKV] = _std_unit(w_in, 16, [512 + ar(128), 640 + ar(128)])
    kx = np.concatenate([768 + ar(64), 768 + ar(64)])
    ky = np.concatenate([800 + ar(32), 768 + ar(32), 800 + ar(32), 768 + ar(32)])
    units[U_KPE] = _std_unit(w_in, 16, [kx, ky])
    units[U_UKV] = _std_unit(w_ukv, 2, [h * 256 + ar(128) for h in range(8)] + [h * 256 + 128 + ar(128) for h in range(8)])

    def ab(base, p):
        A = np.concatenate([base + (2 * p) * 128 + ar(64), base + (2 * p + 1) * 128 + ar(64)])
        Bc = np.concatenate([base + (2 * p) * 128 + 64 + ar(64), base + (2 * p + 1) * 128 + 64 + ar(64)])
        return [A, Bc]
    for p in range(4):
        units[U_RQ + p] = _std_unit(w_in, 16, ab(1856, p))
        units[U_RK + p] = _std_unit(w_in, 16, ab(2880, p))
        units[U_RV + p] = _std_unit(w_in, 16, [3904 + (2 * p) * 128 + ar(128), 3904 + (2 * p + 1) * 128 + ar(128)])
        units[U_ZR + p] = _std_unit(w_in, 16, [4928 + (2 * p) * 128 + ar(128), 4928 + (2 * p + 1) * 128 + ar(128)])
        units[U_ZM + p] = _std_unit(w_in, 16, [832 + (2 * p) * 128 + ar(128), 832 + (2 * p + 1) * 128 + ar(128)])
    for u in range(2):
        units[U_CQ + u] = _std_unit(w_in, 16, [(2 * u) * 128 + ar(128), (2 * u + 1) * 128 + ar(128)])
    units[U_UQ] = _std_unit(w_uq, 4, [h * 192 + ar(128) for h in range(8)])
    xy = []
    for p in range(4):
        X = np.concatenate([(2 * p) * 192 + 128 + ar(64), (2 * p + 1) * 192 + 128 + ar(64)])
        Y = np.concatenate([(2 * p) * 192 + 160 + ar(32), (2 * p) * 192 + 128 + ar(32),
                            (2 * p + 1) * 192 + 160 + ar(32), (2 * p + 1) * 192 + 128 + ar(32)])
        xy += [X, Y]
    units[U_UQ + 1] = _std_unit(w_uq, 4, xy)
    GB = 5952
    for pj in range(8):
        f = 2 * pj
        units[U_G + 3 * pj] = _std_unit(w_in, 16, [GB + f * 128 + ar(128), GB + 2048 + f * 128 + ar(128)])
        units[U_G + 3 * pj + 2] = _std_unit(w_in, 16, [GB + (f + 1) * 128 + ar(128), GB + 2048 + (f + 1) * 128 + ar(128)])
        bmr = np.concatenate([w_bm, w_br], axis=1)
        units[U_G + 3 * pj + 1] = _std_unit(bmr, 8, [f * 128 + ar(128), 2048 + f * 128 + ar(128),
                                                      (f + 1) * 128 + ar(128), 2048 + (f + 1) * 128 + ar(128)])
    for u in range(8):
        units[U_WO + u] = w_out[:, u * 256:(u + 1) * 256].reshape(16, 128, 256).transpose(1, 0, 2).reshape(128, UE)
    return np.ascontiguousarray(np.stack(units).astype(np.float32))


def _const_tables():
    pos_all = np.concatenate([np.arange(16), 16 + np.arange(SEQ)]).astype(np.float32)
    inv_r = (10000.0 ** (-np.arange(0, 128, 2, dtype=np.float32) / 128)).astype(np.float32)
    inv_m = (10000.0 ** (-np.arange(0, 64, 2, dtype=np.float32) / 64)).astype(np.float32)
    tabs = np.zeros((NT + 1, 128, 4, T), np.float32)
    pidx = np.arange(128)
    fr = pidx % 64
    rm = pidx % 64
    fm = rm % 32
    sign = np.where(rm < 32, -1.0, 1.0).astype(np.float32)
    for ti in range(NT + 1):
        if ti == 0:
            pos = np.zeros(T, np.float32)
            pos[:16] = pos_all[:16]
        else:
            pos = pos_all[16 + (ti - 1) * T:16 + ti * T]
        ang_r = (pos[None, :] * inv_r[fr][:, None]).astype(np.float32)
        ang_m = (pos[None, :] * inv_m[fm][:, None]).astype(np.float32)
        tabs[ti, :, 0] = np.cos(ang_r)
        tabs[ti, :, 1] = np.sin(ang_r)
        tabs[ti, :, 2] = np.cos(ang_m)
        tabs[ti, :, 3] = np.sin(ang_m) * sign[:, None]
    log_g = np.log1p(-(2.0 ** (-5.0 - np.arange(8, dtype=np.float64))))
    m = np.arange(128, dtype=np.float64)
    cdec = np.exp(-log_g[None, :] * (m[:, None] + 1.0)) * (128 ** -0.5)
    czeta = np.exp(log_g[None, :] * (127.0 - m[:, None])) * (128 ** -0.5)
    epsx = GN_EPS * np.exp(-2.0 * log_g[None, :] * (m[:, None] + 1.0))
    czm = np.zeros((128, 8))
    czm[:16] = czeta[112:128]
    sqc = np.zeros((2, 128, 128), np.float32)
    sqc[0] = np.eye(128)
    sqc[1] = (m[None, :] >= m[:, None])
    return tabs, cdec, czeta, epsx, czm, sqc


_CACHE = {}


def kernel(x, meta, norm_w, w_in, mla_q_norm_w, mla_w_uq, mla_kv_norm_w, mla_w_ukv,
           ret_gn_w, ret_gn_b, w_branch_mla, w_branch_ret, w_out, final_norm_w):
    f = lambda a: np.asarray(a, dtype=np.float32)
    x = f(x)
    wpack = _build_wpack(f(w_in)[0], f(mla_w_uq)[0], f(mla_w_ukv)[0], f(w_branch_mla)[0], f(w_branch_ret)[0], f(w_out)[0])
    tabs, cdec, czeta, epsx, czm, sqc = _const_tables()
    bcv = np.stack([np.broadcast_to(f(norm_w)[0][None, :], (128, D)), np.broadcast_to(f(final_norm_w)[None, :], (128, D))])
    bcv = np.ascontiguousarray(bcv.astype(np.float32))
    colv = np.zeros((128, 64), np.float32)
    colv[:, 0:4] = f(mla_q_norm_w)[0].reshape(4, 128).T
    colv[:, 4:6] = f(mla_kv_norm_w)[0].reshape(2, 128).T
    colv[:, 8:16] = f(ret_gn_w)[0].reshape(8, 128).T
    colv[:, 16:24] = f(ret_gn_b)[0].reshape(8, 128).T
    colv[:, 24:32] = cdec
    colv[:, 32:40] = czeta
    colv[:, 40:48] = epsx[:, [0, 2, 4, 6, 1, 3, 5, 7]]
    colv[:, 48:56] = czm
    if "nc" not in _CACHE:
        _CACHE["nc"] = build_nc()
    nc = _CACHE["nc"]
    metaf = np.ascontiguousarray(f(meta))
    in_maps = [{"x": np.ascontiguousarray(x[b]), "meta": metaf, "wpack": wpack, "bcv": bcv, "colv": colv,
                "sqc": sqc, "tabs": tabs} for b in range(8)]
    res = run_bass_kernel_spmd(nc, in_maps, core_ids=list(range(8)))
    return np.stack([np.asarray(r["y"], dtype=np.float32) for r in res.results], axis=0)
```

```python
import math
from contextlib import ExitStack

import numpy as np
import concourse.bass as bass
import concourse.mybir as mybir
from concourse.bass_utils import run_bass_kernel_spmd

F32 = mybir.dt.float32
BF16 = mybir.dt.bfloat16
AF = mybir.ActivationFunctionType
ALU = mybir.AluOpType
AX = mybir.AxisListType

SEQ = 4096
D = 2048
NT = 8
T = 512
NKEY = 16 + SEQ
NBLK = 33
NU = 59
UE = 4096
SC = (128 + 64) ** -0.5
NORM_EPS = 1e-6
GN_EPS = 1e-5
U_CKV, U_KPE, U_UKV = 0, 1, 2
U_RQ, U_RK, U_RV, U_ZR = 3, 7, 11, 15
U_ZM, U_CQ, U_UQ = 19, 23, 25
U_G = 27
U_WO = 51


class _Stop(Exception):
    pass


class Buf:
    __slots__ = ("name", "w", "rs", "excl")

    def __init__(self, name, excl=False):
        self.name = name
        self.w = None
        self.rs = {}
        self.excl = excl


class Prog:
    CE = ("pe", "act", "dve", "pool")
    ALL = ("pe", "act", "dve", "pool", "sp")

    def __init__(self, K=8):
        self.ops = {e: [] for e in self.ALL}
        self.cnt = {e: 0 for e in self.CE}
        self.seen = {e: {} for e in self.ALL}
        self.K = K
        self.dcnt = {e: 0 for e in self.ALL}
        self.semkeys = set()
        self.nops = 0
        self.stop_at = None

    def op(self, eng, fn, reads=(), writes=(), dma=False):
        self.nops += 1
        if self.stop_at is not None and self.nops > self.stop_at:
            raise _Stop()
        deps = {}

        def add(ev):
            if ev is None:
                return
            sk, v = ev
            if deps.get(sk, 0) < v:
                deps[sk] = v

        for b in reads:
            add(b.w)
            if b.excl:
                for sk, v in b.rs.items():
                    if sk != eng:
                        add((sk, v))
        for b in writes:
            add(b.w)
            for sk, v in b.rs.items():
                add((sk, v))
        if dma:
            idx = self.dcnt[eng]
            self.dcnt[eng] += 1
            s = idx % self.K
            sk = ("dma", eng, s)
            val = 16 * (idx // self.K + 1)
            if val > 16:
                add((sk, val - 16))
            ev = (sk, val)
        else:
            self.cnt[eng] += 1
            ev = (eng, self.cnt[eng])
        self.semkeys.add(ev[0])
        waits = []
        seen = self.seen[eng]
        for sk, v in deps.items():
            if sk == "pe" and eng == "pe":
                continue
            if seen.get(sk, 0) < v:
                seen[sk] = v
                waits.append((sk, v))
        self.ops[eng].append((fn, waits, ev))
        for b in writes:
            b.w = ev
            b.rs = {}
        for b in reads:
            if b.rs.get(ev[0], 0) < ev[1]:
                b.rs[ev[0]] = ev[1]
        return ev

    def finish(self, eng="sp"):
        waits = []
        for e in self.ALL:
            n = self.dcnt[e]
            for s in range(min(n, self.K)):
                cnt_s = (n - 1 - s) // self.K + 1
                waits.append((("dma", e, s), 16 * cnt_s))
        self.ops[eng].append((None, waits, None))

    def emit(self, nc, sems):
        engs = {"pe": "tensor", "act": "scalar", "dve": "vector", "pool": "gpsimd", "sp": "sync"}
        with nc.Block() as block:
            for e in self.ALL:
                ops = self.ops[e]

                def body(eng, ops=ops):
                    for fn, waits, ev in ops:
                        for sk, v in waits:
                            eng.wait_ge(sems[sk], v)
                        if fn is None:
                            continue
                        ins = fn(eng)
                        ins.then_inc(sems[ev[0]], 16 if ev[0][0] == "dma" else 1)

                getattr(block, engs[e])(body)


def seq(fns):
    def f(e):
        r = None
        for g in fns:
            r = g(e)
        return r
    return f


def MM(out, lhsT, rhs, start, stop):
    return lambda e: e.matmul(out, lhsT=lhsT, rhs=rhs, start=start, stop=stop)


def TR(out, in_, ident):
    return lambda e: e.transpose(out, in_, ident)


def ACT(out, in_, func, **kw):
    return lambda e: e.activation(out=out, in_=in_, func=func, **kw)


def TT(out, in0, in1, op):
    return lambda e: e.tensor_tensor(out=out, in0=in0, in1=in1, op=op)


def TS(out, in0, s1, op0, s2=None, op1=None):
    if op1 is None:
        return lambda e: e.tensor_scalar(out=out, in0=in0, scalar1=s1, scalar2=None, op0=op0)
    return lambda e: e.tensor_scalar(out=out, in0=in0, scalar1=s1, scalar2=s2, op0=op0, op1=op1)


def STT(out, in0, scalar, in1, op0, op1):
    return lambda e: e.scalar_tensor_tensor(out=out, in0=in0, scalar=scalar, in1=in1, op0=op0, op1=op1)


def CP(out, in_):
    return lambda e: e.tensor_copy(out=out, in_=in_)


def RCP(out, in_):
    return lambda e: e.reciprocal(out=out, in_=in_)


def RED(out, in_):
    return lambda e: e.tensor_reduce(out=out, in_=in_, axis=AX.X, op=ALU.add)


def DMA(out, in_):
    return lambda e: e.dma_start(out=out, in_=in_)


def MSET(ap, v):
    return lambda e: e.memset(ap, v)


def build_nc(nt=NT, stop=None):
    nc = bass.Bass("TRN2", target_bir_lowering=False)
    x_d = nc.dram_tensor("x", [SEQ, D], F32, kind="ExternalInput").ap()
    meta_d = nc.dram_tensor("meta", [16, D], F32, kind="ExternalInput").ap()
    wpack_d = nc.dram_tensor("wpack", [NU, 128, UE], F32, kind="ExternalInput").ap()
    bc_d = nc.dram_tensor("bcv", [2, 128, D], F32, kind="ExternalInput").ap()
    cv_d = nc.dram_tensor("colv", [128, 64], F32, kind="ExternalInput").ap()
    sq_d = nc.dram_tensor("sqc", [2, 128, 128], F32, kind="ExternalInput").ap()
    tabs_d = nc.dram_tensor("tabs", [NT + 1, 128, 4, T], F32, kind="ExternalInput").ap()
    y_d = nc.dram_tensor("y", [SEQ, D], F32, kind="ExternalOutput").ap()
    wbf_d = nc.dram_tensor("wbf", [NU, 128, UE], BF16, kind="Internal").ap()
    kc_d = nc.dram_tensor("kcache", [8, 128, NKEY], BF16, kind="Internal").ap()
    vc_d = nc.dram_tensor("vcache", [8, 128, NBLK, 128], BF16, kind="Internal").ap()

    P = Prog()
    with ExitStack() as st:
        def sb(name, shape, dt):
            return st.enter_context(nc.sbuf_tensor(name, shape, dt))

        def ps(name, shape, dt=F32):
            return st.enter_context(nc.psum_tensor(name, shape, dt))

        ring = sb("ring", [128, 3, UE], BF16)
        xnT = sb("xnT", [128, 16, T], BF16)
        kpe = sb("kpe", [128, NKEY], BF16)
        tabs = sb("tabs_sb", [128, 4, T], F32)
        cqsq = sb("cqsq", [128, 2, T], F32)
        bcv = sb("bcv_sb", [128, 2, D], F32)
        colv = sb("colv_sb", [128, 64], F32)
        sqc = sb("sqc_sb", [128, 2, 128], F32)
        ident = sb("ident", [128, 128], BF16)
        tri = sb("tri", [128, 128], BF16)
        onesbf = sb("onesbf", [128, 128], BF16)
        epsv = sb("epsv", [128, 4], F32)
        xst = sb("xst", [128, 2, D], F32)
        xs = sb("xs", [128, D], BF16)
        st1 = sb("st1", [128, 8], F32)
        cqw = sb("cqw", [128, 4, T], BF16)
        ckvw = sb("ckvw", [128, 2, T], BF16)
        sq32 = sb("sq32", [128, T], F32)
        sqhl = sb("sqhl", [128, 2, 2, T], BF16)
        xin = sb("xin", [128, D], F32)
        rstd_q = sb("rstd_q", [128, T], F32)
        rstd_kv = sb("rstd_kv", [128, T], F32)
        zsm = sb("zsm", [128, 8, T], BF16)
        zsr = sb("zsr", [128, 8, T], BF16)
        qrv = sb("qrv", [128, 8, T], BF16)
        qpe = sb("qpe", [128, 4, T], BF16)
        rqkm = sb("rqkm", [128, 16, T], BF16)
        kn_st = sb("kn_st", [128, 2, T], BF16)
        v_st = sb("v_st", [128, 2, 1024], BF16)
        tmp = sb("tmp", [128, 4, T], F32)
        kbuf = sb("kbuf", [128, 2, 1024], BF16)
        vbuf = sb("vbuf", [128, 2, 8, 128], BF16)
        pT = sb("pT", [128, 3, T], BF16)
        k_tok = sb("k_tok", [128, 1024], BF16)
        v_tok = sb("v_tok", [128, 1024], BF16)
        vz_tok = sb("vz_tok", [128, 1024], BF16)
        SmT = sb("SmT", [128, 8, 128], BF16)
        o_n = sb("o_n", [128, 1024], BF16)
        R32 = sb("R32", [128, 8, 128], F32)
        Rbf = sb("Rbf", [128, 8, 128], BF16)
        gst = sb("gst", [128, 8, 8], F32)

        pb = [ps(f"pb{i}", [128, 512]) for i in range(6)]
        pt6 = ps("pt6", [128, 1024], BF16)
        pt7 = ps("pt7", [128, 1024], BF16)

        B = {}

        def bufs(name, n=None, excl=False):
            if n is None:
                B[name] = Buf(name, excl)
            else:
                B[name] = [Buf(f"{name}{i}", excl) for i in range(n)]
            return B[name]

        b_ring = bufs("ring", 3)
        b_wbf = bufs("wbf", NU)
        b_xnT = bufs("xnT", 4)
        b_kpe = bufs("kpe")
        b_tabs = bufs("tabs")
        b_cqsq = bufs("cqsq")
        b_const = bufs("const")
        b_xst = bufs("xst", 2)
        b_xs = bufs("xs")
        b_st1 = bufs("st1")
        b_st2 = bufs("st2")
        b_cqw = bufs("cqw", 4)
        b_ckvw = bufs("ckvw", 2)
        b_sq32 = bufs("sq32")
        b_sqhl = bufs("sqhl", 2)
        b_xin = bufs("xin")
        b_rq = bufs("rstd_q")
        b_rkv = bufs("rstd_kv")
        b_zsm = bufs("zsm", 8)
        b_zsr = bufs("zsr", 8)
        b_qrv = bufs("qrv", 8)
        b_qpe = bufs("qpe", 4)
        b_rqkm = bufs("rqkm", 16)
        b_knst = bufs("knst", 2)
        b_vst = bufs("vst", 2)
        b_tmp = bufs("tmp", 4)
        b_kbuf = bufs("kbuf", 2)
        b_vbuf = bufs("vbuf", 2)
        b_pT = bufs("pT", 3)
        b_ktok = bufs("ktok")
        b_vtok = bufs("vtok")
        b_vz = bufs("vz")
        b_SmT = bufs("SmT", 8)
        b_on = bufs("on", 8)
        b_R32 = bufs("R32", 8)
        b_Rbf = bufs("Rbf")
        b_gst = bufs("gst", 8)
        b_pb = bufs("pb", 6, excl=True)
        b_pt6 = bufs("pt6", excl=True)
        b_pt7 = bufs("pt7", excl=True)
        b_kc = bufs("kc", 8)
        b_vc = bufs("vc", 8)
        b_y = bufs("y")

        P.op("sp", DMA(bcv[:], bc_d.rearrange("k p n -> p k n")), writes=[b_const], dma=True)
        P.op("sp", DMA(colv[:], cv_d), writes=[b_const], dma=True)
        P.op("sp", DMA(sqc[:], sq_d.rearrange("k p n -> p k n")), writes=[b_const], dma=True)
        P.op("dve", CP(ident[:], sqc[:, 0, :]), reads=[b_const], writes=[b_const])
        P.op("dve", CP(tri[:], sqc[:, 1, :]), reads=[b_const], writes=[b_const])
        P.op("pool", MSET(onesbf[:], 1.0), writes=[b_const])
        P.op("pool", MSET(epsv[:, 0:1], NORM_EPS), writes=[b_const])
        P.op("pool", MSET(epsv[:, 1:2], NORM_EPS / (SC * SC)), writes=[b_const])
        P.op("pool", MSET(R32[:], 0.0), writes=b_R32)
        P.op("pool", MSET(Rbf[:], 0.0), writes=[b_Rbf])
        nw_bc = bcv[:, 0, :]
        fnw_bc = bcv[:, 1, :]
        QNW, KVNW, GNW, GNB, CDEC, CZETA, EPSX, CZM = 0, 4, 8, 16, 24, 32, 40, 48

        def cvc(base, j, rows=128):
            return colv[0:rows, base + j:base + j + 1]

        log_g = [math.log1p(-(2.0 ** (-5.0 - h))) for h in range(8)]
        g128 = [math.exp(lg * 128.0) for lg in log_g]

        items = [U_CKV, U_KPE, U_UKV] + list(range(U_RK, U_RK + 4)) + list(range(U_RV, U_RV + 4))
        for _t in range(nt):
            items += list(range(0, U_WO)) + list(range(U_WO, U_WO + 8)) + list(range(U_WO, U_WO + 8))
        S = {"pos": 0, "loaded": 0, "casted": set(), "castpos": 0}

        def stream_next(expect, prefetch=2):
            i = S["pos"]
            assert items[i] == expect, (i, items[i], expect)
            while S["castpos"] < min(len(items), i + 14):
                u = items[S["castpos"]]
                if u not in S["casted"]:
                    S["casted"].add(u)
                    P.op("pool", DMA(wbf_d[u], wpack_d[u]), writes=[b_wbf[u]], dma=True)
                S["castpos"] += 1
            while S["loaded"] < min(len(items), i + prefetch + 1):
                k = S["loaded"]
                u = items[k]
                P.op("sp", DMA(ring[:, k % 3, :], wbf_d[u]), reads=[b_wbf[u]], writes=[b_ring[k % 3]], dma=True)
                S["loaded"] += 1
            S["pos"] += 1
            return i % 3

        acc_i = [0]

        def next_acc():
            acc_i[0] ^= 1
            return acc_i[0]

        def s1_pre(t, cc):
            meta = t < 0
            rows = 16 if meta else 128
            src = meta_d if meta else x_d[T * t + 128 * cc:T * t + 128 * cc + 128, :]
            P.op("sp", DMA(xin[0:rows, :], src), writes=[b_xin], dma=True)
            P.op("pool", MSET(st1[0:rows, 0:1], 0.0), writes=[b_st1])
            P.op("act", ACT(xs[0:rows, :], xin[0:rows, :], AF.Square, accum_out=st1[0:rows, 0:1]),
                 reads=[b_xin], writes=[b_xs, b_st1])
            P.op("act", ACT(st1[0:rows, 1:2], st1[0:rows, 0:1], AF.Sqrt, bias=epsv[0:rows, 0:1], scale=1.0 / D),
                 reads=[b_st1, b_const], writes=[b_st1])
            P.op("dve", RCP(st1[0:rows, 2:3], st1[0:rows, 1:2]), reads=[b_st1], writes=[b_st1])
            P.op("dve", STT(xs[0:rows, :], xin[0:rows, :], st1[0:rows, 2:3], nw_bc[0:rows, :], ALU.mult, ALU.mult),
                 reads=[b_xin, b_st1, b_const], writes=[b_xs])

        def s1_post(t, cc):
            meta = t < 0
            rows = 16 if meta else 128
            P.op("pe", seq([TR(pt6[:, k * 128:k * 128 + rows], xs[0:rows, k * 128:(k + 1) * 128], ident[0:rows, 0:rows])
                            for k in range(8)]), reads=[b_xs, b_const], writes=[b_pt6])
            P.op("pe", seq([TR(pt7[:, k * 128:k * 128 + rows], xs[0:rows, (8 + k) * 128:(9 + k) * 128], ident[0:rows, 0:rows])
                            for k in range(8)]), reads=[b_xs, b_const], writes=[b_pt7])
            P.op("act", ACT(xnT[:, 0:8, cc * 128:cc * 128 + rows],
                            pt6[:].rearrange("p (k n) -> p k n", k=8)[:, :, 0:rows], AF.Copy),
                 reads=[b_pt6], writes=[b_xnT[cc]])
            P.op("dve", CP(xnT[:, 8:16, cc * 128:cc * 128 + rows],
                           pt7[:].rearrange("p (k n) -> p k n", k=8)[:, :, 0:rows]),
                 reads=[b_pt7], writes=[b_xnT[cc]])

        def tile(t):
            if stop is not None and stop[0] == t and stop[1].startswith("op"):
                P.nops = 0
                P.stop_at = int(stop[1][2:])
            meta = t < 0
            N = 16 if meta else T
            nch = 1 if meta else 4
            rows = 16 if meta else 128
            key0 = 0 if meta else 16 + T * t
            ti = 0 if meta else t + 1

            P.op("sp", DMA(tabs[:], tabs_d[ti]), writes=[b_tabs], dma=True)
            cosR, sinR, Cm, Sm = tabs[:, 0, 0:N], tabs[:, 1, 0:N], tabs[:, 2, 0:N], tabs[:, 3, 0:N]

            if meta or t == 0:
                for cc in range(nch):
                    s1_pre(t, cc)
                    s1_post(t, cc)

            if stop == (t, 's1'):
                raise _Stop()
            def inproj(slot, blk, a):
                base = blk * 16 * 128
                P.op("pe", seq([MM(pb[a][:, 0:N], ring[:, slot, base + kc * 128:base + (kc + 1) * 128], xnT[:, kc, 0:N],
                                   kc == 0, kc == 15) for kc in range(16)]),
                     reads=[b_ring[slot]] + b_xnT[0:nch], writes=[b_pb[a]])

            def sumsq_block(a, par, n, first, last):
                P.op("act", ACT(sq32[:, 0:n], pb[a][:, 0:n], AF.Square), reads=[b_pb[a]], writes=[b_sq32])
                P.op("dve", CP(sqhl[:, par, 0, 0:n], sq32[:, 0:n]), reads=[b_sq32], writes=[b_sqhl[par]])
                P.op("dve", TT(sqhl[:, par, 1, 0:n], sq32[:, 0:n], sqhl[:, par, 0, 0:n], ALU.subtract),
                     reads=[b_sq32, b_sqhl[par]], writes=[b_sqhl[par]])
                P.op("pe", seq([MM(pb[4][:, 0:n], onesbf[:], sqhl[:, par, 0, 0:n], first, False),
                                MM(pb[4][:, 0:n], onesbf[:], sqhl[:, par, 1, 0:n], False, last)]),
                     reads=[b_const, b_sqhl[par]], writes=[b_pb[4]])

            slot = stream_next(U_CKV)
            for kc in range(2):
                a = next_acc()
                inproj(slot, kc, a)
                P.op("dve", CP(tmp[:, 2 + kc, 0:N], pb[a][:, 0:N]), reads=[b_pb[a]], writes=[b_tmp[2 + kc]])
                sumsq_block(a, kc, N, kc == 0, kc == 1)
            P.op("act", ACT(rstd_kv[:, 0:N], pb[4][:, 0:N], AF.Sqrt, bias=epsv[:, 0:1], scale=1.0 / 256),
                 reads=[b_pb[4], b_const], writes=[b_rkv])
            P.op("dve", RCP(rstd_kv[:, 0:N], rstd_kv[:, 0:N]), reads=[b_rkv], writes=[b_rkv])
            for kc in range(2):
                P.op("dve", STT(ckvw[:, kc, 0:N], tmp[:, 2 + kc, 0:N], cvc(KVNW, kc), rstd_kv[:, 0:N], ALU.mult, ALU.mult),
                     reads=[b_tmp[2 + kc], b_const, b_rkv], writes=[b_ckvw[kc]])

            if stop == (t, 's2a1'):
                raise _Stop()
            slot = stream_next(U_KPE)
            inproj(slot, 0, 0)
            inproj(slot, 1, 1)
            P.op("dve", TT(tmp[:, 0, 0:N], pb[0][:, 0:N], Cm, ALU.mult), reads=[b_pb[0], b_tabs], writes=[b_tmp[0]])
            P.op("dve", TT(tmp[:, 1, 0:N], pb[1][:, 0:N], Sm, ALU.mult), reads=[b_pb[1], b_tabs], writes=[b_tmp[1]])
            P.op("pool", TT(kpe[:, key0:key0 + N], tmp[:, 0, 0:N], tmp[:, 1, 0:N], ALU.add),
                 reads=[b_tmp[0], b_tmp[1]], writes=[b_kpe])

            if stop == (t, 's2a2'):
                raise _Stop()
            slot = stream_next(U_UKV)
            for h in range(8):
                a = next_acc()
                base = h * 2 * 128
                P.op("pe", seq([MM(pb[a][:, 0:N], ring[:, slot, base + kc * 128:base + (kc + 1) * 128], ckvw[:, kc, 0:N],
                                   kc == 0, kc == 1) for kc in range(2)]),
                     reads=[b_ring[slot]] + b_ckvw, writes=[b_pb[a]])
                P.op("dve", CP(kn_st[:, h % 2, 0:N], pb[a][:, 0:N]), reads=[b_pb[a]], writes=[b_knst[h % 2]])
                P.op("sp", DMA(kc_d[h, :, key0:key0 + N], kn_st[:, h % 2, 0:N]), reads=[b_knst[h % 2]], writes=[b_kc[h]], dma=True)
            if stop == (t, 's2a3'):
                raise _Stop()
            ringv = ring[:, slot, :].rearrange("p (b k c) -> p b k c", b=16, k=2)
            for cc in range(nch):
                for half in range(2):
                    a = 4 + half
                    P.op("pe", seq([MM(pb[a][0:rows, :], ckvw[:, kc, cc * 128:cc * 128 + rows],
                                       ringv[:, 8 + 4 * half:12 + 4 * half, kc, :], kc == 0, kc == 1) for kc in range(2)]),
                         reads=[b_ring[slot]] + b_ckvw, writes=[b_pb[a]])
                    P.op("act", ACT(v_st[0:rows, cc % 2, half * 512:(half + 1) * 512], pb[a][0:rows, :], AF.Copy),
                         reads=[b_pb[a]], writes=[b_vst[cc % 2]])
                blk = 0 if meta else 1 + 4 * t + cc
                for h in range(8):
                    P.op("sp", DMA(vc_d[h, 0:rows, blk, :], v_st[0:rows, cc % 2, h * 128:(h + 1) * 128]),
                         reads=[b_vst[cc % 2]], writes=[b_vc[h]], dma=True)

            if stop == (t, 's2a'):
                raise _Stop()
            rp_i = [0]

            def rope_pair(slot, dst0):
                bA = 2 * (rp_i[0] % 2)
                bB = bA + 1
                rp_i[0] += 1
                inproj(slot, 0, bA)
                inproj(slot, 1, bB)
                P.op("dve", TT(tmp[:, 0, 0:N], pb[bA][:, 0:N], cosR, ALU.mult), reads=[b_pb[bA], b_tabs], writes=[b_tmp[0]])
                P.op("dve", TT(tmp[:, 1, 0:N], pb[bB][:, 0:N], sinR, ALU.mult), reads=[b_pb[bB], b_tabs], writes=[b_tmp[1]])
                P.op("pool", TT(rqkm[:, dst0, 0:N], tmp[:, 0, 0:N], tmp[:, 1, 0:N], ALU.subtract),
                     reads=[b_tmp[0], b_tmp[1]], writes=[b_rqkm[dst0]])
                P.op("dve", TT(tmp[:, 2, 0:N], pb[bB][:, 0:N], cosR, ALU.mult), reads=[b_pb[bB], b_tabs], writes=[b_tmp[2]])
                P.op("dve", TT(tmp[:, 3, 0:N], pb[bA][:, 0:N], sinR, ALU.mult), reads=[b_pb[bA], b_tabs], writes=[b_tmp[3]])
                P.op("pool", TT(rqkm[:, dst0 + 1, 0:N], tmp[:, 2, 0:N], tmp[:, 3, 0:N], ALU.add),
                     reads=[b_tmp[2], b_tmp[3]], writes=[b_rqkm[dst0 + 1]])

            if not meta:
                for p in range(4):
                    rope_pair(stream_next(U_RQ + p), 2 * p)
            for p in range(4):
                rope_pair(stream_next(U_RK + p), 8 + 2 * p)
            for u in range(4):
                slot = stream_next(U_RV + u)
                for bb in range(2):
                    h = 2 * u + bb
                    a = next_acc()
                    inproj(slot, bb, a)
                    P.op("act", ACT(qrv[:, h, 0:N], pb[a][:, 0:N], AF.Copy), reads=[b_pb[a]], writes=[b_qrv[h]])
            def silu_unit(ubase, u, dst, dbufs):
                slot = stream_next(ubase + u)
                for bb in range(2):
                    h = 2 * u + bb
                    a = next_acc()
                    inproj(slot, bb, a)
                    P.op("act", ACT(dst[:, h, :], pb[a][:, :], AF.Silu), reads=[b_pb[a]], writes=[dbufs[h]])

            fillers = []
            if not meta:
                silu_unit(U_ZR, 0, zsr, b_zsr)
                silu_unit(U_ZR, 1, zsr, b_zsr)
                fillers = [lambda: silu_unit(U_ZR, 2, zsr, b_zsr), lambda: silu_unit(U_ZR, 3, zsr, b_zsr)] + \
                          [(lambda u=u: silu_unit(U_ZM, u, zsm, b_zsm)) for u in range(4)]

            def fill():
                if fillers:
                    fillers.pop(0)()

            if stop == (t, 'Rp'):
                raise _Stop()
            for cc in range(nch):
                s0 = cc * 128
                P.op("pe", seq([TR(pt6[0:rows, j * 128:(j + 1) * 128], rqkm[:, 8 + j, s0:s0 + rows], ident[:]) for j in range(8)]),
                     reads=b_rqkm[8:16] + [b_const], writes=[b_pt6])
                P.op("pe", seq([TR(pt7[0:rows, h * 128:(h + 1) * 128], qrv[:, h, s0:s0 + rows], ident[:]) for h in range(8)]),
                     reads=b_qrv + [b_const], writes=[b_pt7])
                P.op("act", ACT(k_tok[0:rows, :], pt6[0:rows, :], AF.Copy), reads=[b_pt6], writes=[b_ktok])
                zb = CZM if meta else CZETA
                for h in range(8):
                    P.op("act", ACT(vz_tok[0:rows, h * 128:(h + 1) * 128], pt7[0:rows, h * 128:(h + 1) * 128], AF.Identity,
                                    scale=cvc(zb, h, rows)), reads=[b_pt7, b_const], writes=[b_vz])
                if not meta:
                    P.op("dve", CP(v_tok[:], pt7[:]), reads=[b_pt7], writes=[b_vtok])
                    for hh in range(2):
                        r0, r1 = hh * 64, hh * 64 + 64
                        fns = []
                        for p in range(4):
                            o = pb[2 + hh][:, p * 128:(p + 1) * 128]
                            fns.append(MM(o, rqkm[r0:r1, 8 + 2 * p, s0:s0 + 128], rqkm[r0:r1, 2 * p, s0:s0 + 128], True, False))
                            fns.append(MM(o, rqkm[r0:r1, 9 + 2 * p, s0:s0 + 128], rqkm[r0:r1, 2 * p + 1, s0:s0 + 128], False, True))
                        P.op("pe", seq(fns), reads=b_rqkm, writes=[b_pb[2 + hh]])
                    for h in range(8):
                        hh, p = h % 2, h // 2
                        P.op("dve", STT(SmT[:, h, :], pb[2 + hh][:, p * 128:(p + 1) * 128], cvc(CDEC, h), tri[:],
                                        ALU.mult, ALU.mult), reads=[b_pb[2 + hh], b_const], writes=[b_SmT[h]])
                    fill()
                    for hh in range(2):
                        r0, r1 = hh * 64, hh * 64 + 64
                        fns = []
                        for p in range(4):
                            h = 2 * p + hh
                            o = pb[4 + hh][:, p * 128:(p + 1) * 128]
                            fns.append(MM(o, SmT[:, h, :], v_tok[:, h * 128:(h + 1) * 128], True, False))
                            fns.append(MM(o, rqkm[r0:r1, 2 * p, s0:s0 + 128], Rbf[r0:r1, 2 * p, :], False, False))
                            fns.append(MM(o, rqkm[r0:r1, 2 * p + 1, s0:s0 + 128], Rbf[r0:r1, 2 * p + 1, :], False, True))
                        P.op("pe", seq(fns), reads=[b_SmT[2 * p + hh] for p in range(4)] + [b_vtok, b_Rbf] + b_rqkm[0:8],
                             writes=[b_pb[4 + hh]])
                    sqo = tmp[:, 0:2, :]
                    for hh in range(2):
                        P.op("dve", RED(gst[:, 0, 4 * hh:4 * hh + 4], pb[4 + hh][:].rearrange("p (h e) -> p h e", h=4)),
                             reads=[b_pb[4 + hh]], writes=[b_gst[0]])
                        P.op("act", ACT(tmp[:, hh, :], pb[4 + hh][:], AF.Square), reads=[b_pb[4 + hh]], writes=[b_tmp[hh]])
                    P.op("dve", RED(gst[:, 1, :], sqo.rearrange("p a (h e) -> p (a h) e", h=4)), reads=b_tmp[0:2], writes=[b_gst[1]])
                    P.op("dve", TS(gst[:, 2, :], gst[:, 0, :], 1.0 / 128, ALU.mult), reads=[b_gst[0]], writes=[b_gst[2]])
                    P.op("dve", TT(gst[:, 3, :], gst[:, 2, :], gst[:, 2, :], ALU.mult), reads=[b_gst[2]], writes=[b_gst[3]])
                    P.op("dve", STT(gst[:, 4, :], gst[:, 1, :], 1.0 / 128, gst[:, 3, :], ALU.mult, ALU.subtract),
                         reads=[b_gst[1], b_gst[3]], writes=[b_gst[4]])
                    P.op("dve", TT(gst[:, 4, :], gst[:, 4, :], colv[:, EPSX:EPSX + 8], ALU.add), reads=[b_gst[4], b_const], writes=[b_gst[4]])
                    P.op("act", ACT(gst[:, 5, :], gst[:, 4, :], AF.Sqrt), reads=[b_gst[4]], writes=[b_gst[5]])
                    P.op("dve", RCP(gst[:, 6, :], gst[:, 5, :]), reads=[b_gst[5]], writes=[b_gst[6]])
                    P.op("dve", STT(gst[:, 7, :], gst[:, 2, :], -1.0, gst[:, 6, :], ALU.mult, ALU.mult),
                         reads=[b_gst[2], b_gst[6]], writes=[b_gst[7]])
                    for h in range(8):
                        hh, p = h % 2, h // 2
                        i = hh * 4 + p
                        P.op("act", ACT(o_n[:, h * 128:(h + 1) * 128], pb[4 + hh][:, p * 128:(p + 1) * 128], AF.Identity,
                                        scale=gst[:, 6, i:i + 1], bias=gst[:, 7, i:i + 1]),
                             reads=[b_pb[4 + hh], b_gst[6], b_gst[7]], writes=[b_on[h]])
                    fill()
                    P.op("pe", seq([TR(pt6[:, h * 128:(h + 1) * 128], o_n[:, h * 128:(h + 1) * 128], ident[:]) for h in range(8)]),
                         reads=b_on + [b_const], writes=[b_pt6])
                    ont = tmp[:, 2:4, :].rearrange("p a (h e) -> p (a h) e", h=4)
                    for h in range(8):
                        P.op("act", ACT(ont[:, h, :], pt6[:, h * 128:(h + 1) * 128], AF.Identity,
                                        scale=cvc(GNW, h), bias=cvc(GNB, h)),
                             reads=[b_pt6, b_const], writes=[b_tmp[2 + h // 4]])
                    P.op("pool", TT(zsr[:, :, s0:s0 + 128], ont, zsr[:, :, s0:s0 + 128], ALU.mult),
                         reads=b_tmp[2:4] + b_zsr, writes=b_zsr)
                for j in range(8):
                    p = j // 2
                    a = j // 2
                    o = pb[a][:, (j % 2) * 256:(j % 2) * 256 + 256]
                    P.op("pe", MM(o, k_tok[0:rows, j * 128:(j + 1) * 128], vz_tok[0:rows, p * 256:(p + 1) * 256], True, True),
                         reads=[b_ktok, b_vz], writes=[b_pb[a]])
                for j in range(8):
                    p = j // 2
                    a = j // 2
                    for hh in range(2):
                        h = 2 * p + hh
                        r0, r1 = hh * 64, hh * 64 + 64
                        src = pb[a][r0:r1, (j % 2) * 256 + hh * 128:(j % 2) * 256 + hh * 128 + 128]
                        P.op("dve", STT(R32[r0:r1, j, :], R32[r0:r1, j, :], g128[h], src, ALU.mult, ALU.add),
                             reads=[b_pb[a], b_R32[j]], writes=[b_R32[j]])
                P.op("dve", CP(Rbf[:], R32[:]), reads=b_R32, writes=[b_Rbf])
            if stop == (t, 'R'):
                raise _Stop()
            while fillers:
                fill()
            if meta:
                return

            for u in range(2):
                slot = stream_next(U_CQ + u)
                for bb in range(2):
                    j = 2 * u + bb
                    a = next_acc()
                    inproj(slot, bb, a)
                    P.op("dve", TS(cqw[:, j, :], pb[a][:], cvc(QNW, j), ALU.mult), reads=[b_pb[a], b_const], writes=[b_cqw[j]])
                    sumsq_block(a, j % 2, T, j == 0, j == 3)
            P.op("act", ACT(rstd_q[:], pb[4][:], AF.Sqrt, bias=epsv[:, 1:2], scale=1.0 / (512 * SC * SC)),
                 reads=[b_pb[4], b_const], writes=[b_rq])
            P.op("dve", RCP(rstd_q[:], rstd_q[:]), reads=[b_rq], writes=[b_rq])
            P.op("pool", TT(cqsq[:, 0, :], tabs[:, 2, :], rstd_q[:], ALU.mult), reads=[b_tabs, b_rq], writes=[b_cqsq])
            P.op("pool", TT(cqsq[:, 1, :], tabs[:, 3, :], rstd_q[:], ALU.mult), reads=[b_tabs, b_rq], writes=[b_cqsq])

            def uqproj(slot, blk, a):
                base = blk * 4 * 128
                P.op("pe", seq([MM(pb[a][:], ring[:, slot, base + kc * 128:base + (kc + 1) * 128], cqw[:, kc, :], kc == 0, kc == 3)
                                for kc in range(4)]), reads=[b_ring[slot]] + b_cqw, writes=[b_pb[a]])

            slot = stream_next(U_UQ)
            for h in range(8):
                a = next_acc()
                uqproj(slot, h, a)
                P.op("dve", TT(qrv[:, h, :], pb[a][:], rstd_q[:], ALU.mult), reads=[b_pb[a], b_rq], writes=[b_qrv[h]])
            slot = stream_next(U_UQ + 1)
            for p in range(4):
                uqproj(slot, 2 * p, 0)
                uqproj(slot, 2 * p + 1, 1)
                P.op("dve", TT(tmp[:, 0, :], pb[0][:], cqsq[:, 0, :], ALU.mult), reads=[b_pb[0], b_cqsq], writes=[b_tmp[0]])
                P.op("dve", TT(tmp[:, 1, :], pb[1][:], cqsq[:, 1, :], ALU.mult), reads=[b_pb[1], b_cqsq], writes=[b_tmp[1]])
                P.op("pool", TT(qpe[:, p, :], tmp[:, 0, :], tmp[:, 1, :], ALU.add), reads=b_tmp[0:2], writes=[b_qpe[p]])

            if stop == (t, 's2b'):
                raise _Stop()
            nblk = 4 * t + 5
            units = []
            gi = [0]
            for h in range(8):
                for g in range((nblk + 7) // 8):
                    b0, b1 = 8 * g, min(8 * g + 8, nblk)
                    c0 = 0 if g == 0 else 16 + 128 * (b0 - 1)
                    c1 = 16 + 128 * (b1 - 1)
                    for j in range(b0, b1):
                        units.append((h, g, j, b0, b1, c0, c1))
            loaded = {}

            def load_group(h, g, b0, b1, c0, c1):
                kb_i = gi[0] % 2
                gi[0] += 1
                P.op("sp", DMA(kbuf[:, kb_i, 0:c1 - c0], kc_d[h, :, c0:c1]), reads=[b_kc[h]], writes=[b_kbuf[kb_i]], dma=True)
                if b0 == 0:
                    P.op("sp", DMA(vbuf[0:16, kb_i, 0, :], vc_d[h, 0:16, 0, :]), reads=[b_vc[h]], writes=[b_vbuf[kb_i]], dma=True)
                    P.op("sp", DMA(vbuf[:, kb_i, 1:b1, :], vc_d[h, :, 1:b1, :]), reads=[b_vc[h]], writes=[b_vbuf[kb_i]], dma=True)
                else:
                    P.op("sp", DMA(vbuf[:, kb_i, 0:b1 - b0, :], vc_d[h, :, b0:b1, :]), reads=[b_vc[h]], writes=[b_vbuf[kb_i]], dma=True)
                loaded[(h, g)] = kb_i

            def emit_S(ui):
                h, g, j, b0, b1, c0, c1 = units[ui]
                if (h, g) not in loaded:
                    load_group(h, g, b0, b1, c0, c1)
                kb_i = loaded[(h, g)]
                nk = 16 if j == 0 else 128
                kc0 = 0 if j == 0 else 16 + 128 * (j - 1) - c0
                kabs = 0 if j == 0 else 16 + 128 * (j - 1)
                kb = j - (4 * t + 1)
                q0 = 128 * kb if kb >= 0 else 0
                sb_i = ui % 4
                hh = h % 2
                r0, r1 = hh * 64, hh * 64 + 64
                P.op("pe", seq([
                    MM(pb[sb_i][0:nk, q0:T], kbuf[:, kb_i, kc0:kc0 + nk], qrv[:, h, q0:T], True, False),
                    MM(pb[sb_i][0:nk, q0:T], kpe[r0:r1, kabs:kabs + nk], qpe[r0:r1, h // 2, q0:T], False, True)]),
                    reads=[b_kbuf[kb_i], b_qrv[h], b_kpe, b_qpe[h // 2]], writes=[b_pb[sb_i]])

            def emit_PV(ui):
                h, g, j, b0, b1, c0, c1 = units[ui]
                kb_i = loaded[(h, g)]
                nk = 16 if j == 0 else 128
                kb = j - (4 * t + 1)
                q0 = 128 * kb if kb >= 0 else 0
                sb_i = ui % 4
                pi = ui % 3
                po, psm = 4, 5
                P.op("act", ACT(pT[0:nk, pi, q0:T], pb[sb_i][0:nk, q0:T], AF.Exp), reads=[b_pb[sb_i]], writes=[b_pT[pi]])
                if kb >= 0:
                    P.op("dve", TT(pT[:, pi, q0:q0 + 128], pT[:, pi, q0:q0 + 128], tri[:], ALU.mult),
                         reads=[b_pT[pi], b_const], writes=[b_pT[pi]])
                first, last = (j == 0), (j == nblk - 1)
                P.op("pe", seq([
                    MM(pb[po][:, q0:T], vbuf[0:nk, kb_i, j - b0, :], pT[0:nk, pi, q0:T], first, last),
                    MM(pb[psm][:, q0:T], onesbf[0:nk, :], pT[0:nk, pi, q0:T], first, last)]),
                    reads=[b_vbuf[kb_i], b_pT[pi], b_const], writes=[b_pb[po], b_pb[psm]])
                if last:
                    P.op("dve", RCP(tmp[:, 0, :], pb[psm][:]), reads=[b_pb[psm]], writes=[b_tmp[0]])
                    P.op("dve", TT(tmp[:, 1, :], pb[po][:], tmp[:, 0, :], ALU.mult), reads=[b_pb[po], b_tmp[0]], writes=[b_tmp[1]])
                    P.op("pool", TT(zsm[:, h, :], tmp[:, 1, :], zsm[:, h, :], ALU.mult), reads=[b_tmp[1], b_zsm[h]], writes=[b_zsm[h]])

            DEPTH = 3
            gkeys = []
            for (h, g, j, b0, b1, c0, c1) in units:
                if not gkeys or gkeys[-1] != (h, g):
                    gkeys.append((h, g))
            gseq = {k: i for i, k in enumerate(gkeys)}
            last_unit = {}
            for ui, (h, g, j, b0, b1, c0, c1) in enumerate(units):
                last_unit[gseq[(h, g)]] = ui
            st_ = {"next_S": 0, "pv_done": 0}

            def can_emit(k):
                h, g = units[k][0], units[k][1]
                if (h, g) in loaded:
                    return True
                q = gseq[(h, g)] - 2
                return q < 0 or last_unit[q] < st_["pv_done"]

            def pump(limit):
                while st_["next_S"] < min(limit, len(units)) and can_emit(st_["next_S"]):
                    emit_S(st_["next_S"])
                    st_["next_S"] += 1

            for ui in range(len(units)):
                pump(ui + DEPTH + 1)
                assert st_["next_S"] > ui
                emit_PV(ui)
                st_["pv_done"] = ui + 1

            if stop == (t, 'A'):
                raise _Stop()
            def brproj(slot, blk, a, src, srcbufs):
                base = blk * 8 * 128
                P.op("pe", seq([MM(pb[a][:], ring[:, slot, base + kc * 128:base + (kc + 1) * 128], src[:, kc, :], kc == 0, kc == 7)
                                for kc in range(8)]), reads=[b_ring[slot]] + srcbufs, writes=[b_pb[a]])

            def merge(f):
                P.op("act", ACT(tmp[:, 0, :], pb[0][:], AF.Sigmoid), reads=[b_pb[0]], writes=[b_tmp[0]])
                P.op("act", ACT(tmp[:, 1, :], pb[1][:], AF.Sigmoid), reads=[b_pb[1]], writes=[b_tmp[1]])
                P.op("dve", TT(tmp[:, 2, :], tmp[:, 0, :], pb[2][:], ALU.mult), reads=[b_tmp[0], b_pb[2]], writes=[b_tmp[2]])
                P.op("dve", TT(tmp[:, 3, :], tmp[:, 1, :], pb[3][:], ALU.mult), reads=[b_tmp[1], b_pb[3]], writes=[b_tmp[3]])
                P.op("pool", TT(rqkm[:, f, :], tmp[:, 2, :], tmp[:, 3, :], ALU.add), reads=b_tmp[2:4], writes=[b_rqkm[f]])

            for pj in range(8):
                f = 2 * pj
                sA = stream_next(U_G + 3 * pj)
                inproj(sA, 0, 0)
                inproj(sA, 1, 1)
                sC = stream_next(U_G + 3 * pj + 1)
                brproj(sC, 0, 2, zsm, b_zsm)
                brproj(sC, 1, 3, zsr, b_zsr)
                merge(f)
                sB = stream_next(U_G + 3 * pj + 2, prefetch=1)
                inproj(sB, 0, 0)
                inproj(sB, 1, 1)
                brproj(sC, 2, 2, zsm, b_zsm)
                brproj(sC, 3, 3, zsr, b_zsr)
                merge(f + 1)

            if stop == (t, 'G'):
                raise _Stop()
            for pp in range(2):
                for c2 in range(2):
                    cc = 2 * pp + c2
                    P.op("sp", DMA(xst[:, c2, :], x_d[T * t + 128 * cc:T * t + 128 * cc + 128, :]), writes=[b_xst[c2]], dma=True)
                oi = 0
                nxt = (pp == 0 and t + 1 < nt)
                if nxt:
                    s1_pre(t + 1, 0)
                for u in range(8):
                    if nxt and u in (2, 4, 6):
                        s1_post(t + 1, u // 2 - 1)
                        s1_pre(t + 1, u // 2)
                    slot = stream_next(U_WO + u)
                    for c2 in range(2):
                        cc = 2 * pp + c2
                        a = oi % 4
                        oi += 1
                        P.op("pe", seq([MM(pb[a][:, 0:256], rqkm[:, kc, cc * 128:(cc + 1) * 128], ring[:, slot, kc * 256:(kc + 1) * 256],
                                           kc == 0, kc == 15) for kc in range(16)]),
                             reads=[b_ring[slot]] + b_rqkm, writes=[b_pb[a]])
                        P.op("dve", TT(xst[:, c2, u * 256:(u + 1) * 256], pb[a][:, 0:256], xst[:, c2, u * 256:(u + 1) * 256], ALU.add),
                             reads=[b_pb[a], b_xst[c2]], writes=[b_xst[c2]])
                if nxt:
                    s1_post(t + 1, 3)
                for c2 in range(2):
                    cc = 2 * pp + c2
                    P.op("pool", MSET(st1[:, 4:5], 0.0), writes=[b_st2])
                    P.op("act", ACT(kbuf[:].rearrange("p a n -> p (a n)"), xst[:, c2, :], AF.Square, accum_out=st1[:, 4:5]),
                         reads=[b_xst[c2]], writes=b_kbuf + [b_st2])
                    P.op("act", ACT(st1[:, 5:6], st1[:, 4:5], AF.Sqrt, bias=epsv[:, 0:1], scale=1.0 / D),
                         reads=[b_st2, b_const], writes=[b_st2])
                    P.op("dve", RCP(st1[:, 6:7], st1[:, 5:6]), reads=[b_st2], writes=[b_st2])
                    P.op("dve", STT(xst[:, c2, :], xst[:, c2, :], st1[:, 6:7], fnw_bc, ALU.mult, ALU.mult),
                         reads=[b_xst[c2], b_st2, b_const], writes=[b_xst[c2]])
                    P.op("sp", DMA(y_d[T * t + 128 * cc:T * t + 128 * cc + 128, :], xst[:, c2, :]),
                         reads=[b_xst[c2]], writes=[b_y], dma=True)

        try:
            tile(-1)
            for t in range(nt):
                tile(t)
            assert S["pos"] == len(items), (S["pos"], len(items))
        except _Stop:
            pass
        P.finish()

        sems = {}
        for sk in sorted(P.semkeys, key=str):
            nm = "s_" + "".join(ch if ch.isalnum() else "_" for ch in str(sk))
            sems[sk] = st.enter_context(nc.semaphore(nm))
        P.emit(nc, sems)
    return nc


def _std_unit(W, KC, col_lists):
    blks = []
    for cols in col_lists:
        blk = W[:, cols].reshape(KC, 128, 128).transpose(1, 0, 2)
        blks.append(blk.reshape(128, KC * 128))
    u = np.concatenate(blks, axis=1)
    assert u.shape == (128, UE), u.shape
    return u


def _build_wpack(w_in, w_uq, w_ukv, w_bm, w_br, w_out):
    ar = np.arange
    units = [None] * NU
    units[U_CKV] = _std_unit(w_in, 16, [512 + ar(128), 640 + ar(128)])
    kx = np.concatenate([768 + ar(64), 768 + ar(64)])
    ky = np.concatenate([800 + ar(32), 768 + ar(32), 800 + ar(32), 768 + ar(32)])
    units[U_KPE] = _std_unit(w_in, 16, [kx, ky])
    units[U_UKV] = _std_unit(w_ukv, 2, [h * 256 + ar(128) for h in range(8)] + [h * 256 + 128 + ar(128) for h in range(8)])

    def ab(base, p):
        A = np.concatenate([base + (2 * p) * 128 + ar(64), base + (2 * p + 1) * 128 + ar(64)])
        Bc = np.concatenate([base + (2 * p) * 128 + 64 + ar(64), base + (2 * p + 1) * 128 + 64 + ar(64)])
        return [A, Bc]
    for p in range(4):
        units[U_RQ + p] = _std_unit(w_in, 16, ab(1856, p))
        units[U_RK + p] = _std_unit(w_in, 16, ab(2880, p))
        units[U_RV + p] = _std_unit(w_in, 16, [3904 + (2 * p) * 128 + ar(128), 3904 + (2 * p + 1) * 128 + ar(128)])
        units[U_ZR + p] = _std_unit(w_in, 16, [4928 + (2 * p) * 128 + ar(128), 4928 + (2 * p + 1) * 128 + ar(128)])
        units[U_ZM + p] = _std_unit(w_in, 16, [832 + (2 * p) * 128 + ar(128), 832 + (2 * p + 1) * 128 + ar(128)])
    for u in range(2):
        units[U_CQ + u] = _std_unit(w_in, 16, [(2 * u) * 128 + ar(128), (2 * u + 1) * 128 + ar(128)])
    units[U_UQ] = _std_unit(w_uq, 4, [h * 192 + ar(128) for h in range(8)])
    xy = []
    for p in range(4):
        X = np.concatenate([(2 * p) * 192 + 128 + ar(64), (2 * p + 1) * 192 + 128 + ar(64)])
        Y = np.concatenate([(2 * p) * 192 + 160 + ar(32), (2 * p) * 192 + 128 + ar(32),
                            (2 * p + 1) * 192 + 160 + ar(32), (2 * p + 1) * 192 + 128 + ar(32)])
        xy += [X, Y]
    units[U_UQ + 1] = _std_unit(w_uq, 4, xy)
    GB = 5952
    for pj in range(8):
        f = 2 * pj
        units[U_G + 3 * pj] = _std_unit(w_in, 16, [GB + f * 128 + ar(128), GB + 2048 + f * 128 + ar(128)])
        units[U_G + 3 * pj + 2] = _std_unit(w_in, 16, [GB + (f + 1) * 128 + ar(128), GB + 2048 + (f + 1) * 128 + ar(128)])
        bmr = np.concatenate([w_bm, w_br], axis=1)
        units[U_G + 3 * pj + 1] = _std_unit(bmr, 8, [f * 128 + ar(128), 2048 + f * 128 + ar(128),
                                                      (f + 1) * 128 + ar(128), 2048 + (f + 1) * 128 + ar(128)])
    for u in range(8):
        units[U_WO + u] = w_out[:, u * 256:(u + 1) * 256].reshape(16, 128, 256).transpose(1, 0, 2).reshape(128, UE)
    return np.ascontiguousarray(np.stack(units).astype(np.float32))


def _const_tables():
    pos_all = np.concatenate([np.arange(16), 16 + np.arange(SEQ)]).astype(np.float32)
    inv_r = (10000.0 ** (-np.arange(0, 128, 2, dtype=np.float32) / 128)).astype(np.float32)
    inv_m = (10000.0 ** (-np.arange(0, 64, 2, dtype=np.float32) / 64)).astype(np.float32)
    tabs = np.zeros((NT + 1, 128, 4, T), np.float32)
    pidx = np.arange(128)
    fr = pidx % 64
    rm = pidx % 64
    fm = rm % 32
    sign = np.where(rm < 32, -1.0, 1.0).astype(np.float32)
    for ti in range(NT + 1):
        if ti == 0:
            pos = np.zeros(T, np.float32)
            pos[:16] = pos_all[:16]
        else:
            pos = pos_all[16 + (ti - 1) * T:16 + ti * T]
        ang_r = (pos[None, :] * inv_r[fr][:, None]).astype(np.float32)
        ang_m = (pos[None, :] * inv_m[fm][:, None]).astype(np.float32)
        tabs[ti, :, 0] = np.cos(ang_r)
        tabs[ti, :, 1] = np.sin(ang_r)
        tabs[ti, :, 2] = np.cos(ang_m)
        tabs[ti, :, 3] = np.sin(ang_m) * sign[:, None]
    log_g = np.log1p(-(2.0 ** (-5.0 - np.arange(8, dtype=np.float64))))
    m = np.arange(128, dtype=np.float64)
    cdec = np.exp(-log_g[None, :] * (m[:, None] + 1.0)) * (128 ** -0.5)
    czeta = np.exp(log_g[None, :] * (127.0 - m[:, None])) * (128 ** -0.5)
    epsx = GN_EPS * np.exp(-2.0 * log_g[None, :] * (m[:, None] + 1.0))
    czm = np.zeros((128, 8))
    czm[:16] = czeta[112:128]
    sqc = np.zeros((2, 128, 128), np.float32)
    sqc[0] = np.eye(128)
    sqc[1] = (m[None, :] >= m[:, None])
    return tabs, cdec, czeta, epsx, czm, sqc


_CACHE = {}


def kernel(x, meta, norm_w, w_in, mla_q_norm_w, mla_w_uq, mla_kv_norm_w, mla_w_ukv,
           ret_gn_w, ret_gn_b, w_branch_mla, w_branch_ret, w_out, final_norm_w):
    f = lambda a: np.asarray(a, dtype=np.float32)
    x = f(x)
    wpack = _build_wpack(f(w_in)[0], f(mla_w_uq)[0], f(mla_w_ukv)[0], f(w_branch_mla)[0], f(w_branch_ret)[0], f(w_out)[0])
    tabs, cdec, czeta, epsx, czm, sqc = _const_tables()
    bcv = np.stack([np.broadcast_to(f(norm_w)[0][None, :], (128, D)), np.broadcast_to(f(final_norm_w)[None, :], (128, D))])
    bcv = np.ascontiguousarray(bcv.astype(np.float32))
    colv = np.zeros((128, 64), np.float32)
    colv[:, 0:4] = f(mla_q_norm_w)[0].reshape(4, 128).T
    colv[:, 4:6] = f(mla_kv_norm_w)[0].reshape(2, 128).T
    colv[:, 8:16] = f(ret_gn_w)[0].reshape(8, 128).T
    colv[:, 16:24] = f(ret_gn_b)[0].reshape(8, 128).T
    colv[:, 24:32] = cdec
    colv[:, 32:40] = czeta
    colv[:, 40:48] = epsx[:, [0, 2, 4, 6, 1, 3, 5, 7]]
    colv[:, 48:56] = czm
    if "nc" not in _CACHE:
        _CACHE["nc"] = build_nc()
    nc = _CACHE["nc"]
    metaf = np.ascontiguousarray(f(meta))
    in_maps = [{"x": np.ascontiguousarray(x[b]), "meta": metaf, "wpack": wpack, "bcv": bcv, "colv": colv,
                "sqc": sqc, "tabs": tabs} for b in range(8)]
    res = run_bass_kernel_spmd(nc, in_maps, core_ids=list(range(8)))
    return np.stack([np.asarray(r["y"], dtype=np.float32) for r in res.results], axis=0)
```
